# Optimizing a Trainium2 kernel written in Bass

```python
import jax, jax.numpy as jnp
from jax import lax
import numpy as np

D_MODEL = 1024
BATCH = 2
SEQ = 16384
DEPTH = 2

N_MIXERS = 2
PLE_DIM = 256
ATT_HEADS = 16
ATT_HEAD_DIM = 64
ATT_WIDTH = ATT_HEADS * ATT_HEAD_DIM
QUERY_BLOCK = 128
REC_HEADS = 8
REC_KEY_DIM = 128
REC_VAL_DIM = 128
REC_WIDTH = REC_HEADS * REC_KEY_DIM
REC_VWIDTH = REC_HEADS * REC_VAL_DIM
CHUNK = 64
N_ATT_LAYERS = (DEPTH + 1) // 2
N_REC_LAYERS = DEPTH // 2
EPS = 1e-6

kernel_name = "fox_hgrn2_interleaved_hybrid"


def rms_norm(x, gain):
    xf = x.astype(jnp.float32)
    y = xf * lax.rsqrt(jnp.mean(xf * xf, axis=-1, keepdims=True) + EPS)
    return (y * gain.astype(jnp.float32)).astype(x.dtype)


def fox_attention(q, k, v, c):
    B, H, S, hd = q.shape
    scale = hd ** -0.5
    kpos = jnp.arange(S)

    def block(i):
        start = i * QUERY_BLOCK
        qb = lax.dynamic_slice_in_dim(q, start, QUERY_BLOCK, axis=2)
        cb = lax.dynamic_slice_in_dim(c, start, QUERY_BLOCK, axis=2)
        s = jnp.einsum('bhqd,bhkd->bhqk', qb, k).astype(jnp.float32) * scale
        s = s + (cb[..., :, None] - c[..., None, :])
        qpos = start + jnp.arange(QUERY_BLOCK)
        s = jnp.where(kpos[None, :] <= qpos[:, None], s, -jnp.inf)
        w = jax.nn.softmax(s, axis=-1)
        return jnp.einsum('bhqk,bhkd->bhqd', w.astype(v.dtype), v)

    out = lax.map(block, jnp.arange(S // QUERY_BLOCK))
    return out.transpose(1, 0, 3, 2, 4).reshape(B, S, H * hd)


def fox_mixer(u, w_in, b_f, w_out):
    B, S, _ = u.shape
    proj = u @ w_in
    q, k, v, g, fl = jnp.split(proj, [ATT_WIDTH, 2 * ATT_WIDTH, 3 * ATT_WIDTH, 4 * ATT_WIDTH], axis=-1)
    heads = lambda t: t.reshape(B, S, ATT_HEADS, ATT_HEAD_DIM).transpose(0, 2, 1, 3)
    log_f = jax.nn.log_sigmoid(fl.astype(jnp.float32) + b_f.astype(jnp.float32))
    c = jnp.cumsum(log_f, axis=1).transpose(0, 2, 1)
    attn = fox_attention(heads(q), heads(k), heads(v), c)
    return (attn * jax.nn.silu(g)) @ w_out


def hgrn2_mixer(u, w_in, lb, out_gain, w_out):
    B, S, _ = u.shape
    proj = u @ w_in
    q, fl, inp, g = jnp.split(proj, [REC_WIDTH, 2 * REC_WIDTH, 2 * REC_WIDTH + REC_VWIDTH], axis=-1)
    lbf = lb.astype(jnp.float32)
    log_f = jnp.logaddexp(jnp.log(lbf), jnp.log1p(-lbf) + jax.nn.log_sigmoid(fl.astype(jnp.float32)))
    k = -jnp.expm1(log_f)

    def chunks(t, dh):
        return t.astype(jnp.float32).reshape(B, S // CHUNK, CHUNK, REC_HEADS, dh).transpose(1, 0, 3, 2, 4)

    qc, kc, gc = chunks(q, REC_KEY_DIM), chunks(k, REC_KEY_DIM), chunks(log_f, REC_KEY_DIM)
    ic = chunks(inp, REC_VAL_DIM)
    causal = jnp.tril(jnp.ones((CHUNK, CHUNK), dtype=bool))

    def step(state, xs):
        qb, kb, gb, ib = xs
        b = jnp.cumsum(gb, axis=-2)
        diff = b[..., :, None, :] - b[..., None, :, :]
        decay = jnp.exp(jnp.where(causal[:, :, None], diff, -jnp.inf))
        scores = jnp.einsum('bhtd,bhsd,bhtsd->bhts', qb, kb, decay)
        o = scores @ ib + jnp.einsum('bhtd,bhdv->bhtv', qb * jnp.exp(b), state)
        b_last = b[..., -1:, :]
        state = jnp.exp(b_last[..., 0, :])[..., None] * state + \
            jnp.einsum('bhsd,bhsv->bhdv', kb * jnp.exp(b_last - b), ib)
        return state, o

    state0 = jnp.zeros((B, REC_HEADS, REC_KEY_DIM, REC_VAL_DIM), jnp.float32)
    _, o = lax.scan(step, state0, (qc, kc, gc, ic))
    o = o.transpose(1, 0, 3, 2, 4).reshape(B, S, REC_HEADS, REC_VAL_DIM)
    o = o * lax.rsqrt(jnp.mean(o * o, axis=-1, keepdims=True) + EPS) * out_gain.astype(jnp.float32)
    gh = g.astype(jnp.float32).reshape(B, S, REC_HEADS, REC_VAL_DIM)
    y = (o * jax.nn.silu(gh)).reshape(B, S, REC_VWIDTH).astype(u.dtype)
    return y @ w_out


def setup_inputs(seed: int = 0) -> dict:
    key = jax.random.key(seed)
    ks = jax.random.split(key, 14)
    nrm = jax.random.normal
    D = D_MODEL
    return {
        "x": nrm(ks[0], (BATCH, SEQ, D), jnp.float32),
        "p": nrm(ks[1], (DEPTH, BATCH, SEQ, PLE_DIM), jnp.float32),
        "norm_pre": 1.0 + 0.05 * nrm(ks[2], (DEPTH, D), jnp.float32),
        "norm_post": 1.0 + 0.05 * nrm(ks[3], (DEPTH, D), jnp.float32),
        "att_w_in": nrm(ks[4], (N_ATT_LAYERS, D, 4 * ATT_WIDTH + ATT_HEADS), jnp.float32) * D ** -0.5,
        "att_b_f": jax.random.uniform(ks[5], (N_ATT_LAYERS, ATT_HEADS), jnp.float32, 1.0, 6.0),
        "att_w_out": nrm(ks[6], (N_ATT_LAYERS, ATT_WIDTH, D), jnp.float32) * ATT_WIDTH ** -0.5,
        "rec_w_in": nrm(ks[7], (N_REC_LAYERS, D, 2 * REC_WIDTH + 2 * REC_VWIDTH), jnp.float32) * D ** -0.5,
        "rec_lb": 1.0 + 0.1 * nrm(ks[8], (DEPTH, REC_WIDTH), jnp.float32),
        "rec_out_norm": 1.0 + 0.05 * nrm(ks[9], (N_REC_LAYERS, REC_VAL_DIM), jnp.float32),
        "rec_w_out": nrm(ks[10], (N_REC_LAYERS, REC_VWIDTH, D), jnp.float32) * REC_VWIDTH ** -0.5,
        "ple_w_proj": nrm(ks[11], (DEPTH, PLE_DIM, D), jnp.float32) * PLE_DIM ** -0.5,
        "ple_w_gate": nrm(ks[12], (DEPTH, D, D), jnp.float32) * D ** -0.5,
    }


def reference(x, p, norm_pre, norm_post, att_w_in, att_b_f, att_w_out, rec_w_in, rec_lb,
              rec_out_norm, rec_w_out, ple_w_proj, ple_w_gate):
    sm = jax.nn.softmax(rec_lb.astype(jnp.float32), axis=0)
    lower_bounds = jnp.cumsum(sm, axis=0) - sm[0:1]
    h = x
    for layer in range(DEPTH):
        u = rms_norm(h, norm_pre[layer])
        j = layer // N_MIXERS
        if layer % N_MIXERS == 0:
            y = fox_mixer(u, att_w_in[j], att_b_f[j], att_w_out[j])
        else:
            y = hgrn2_mixer(u, rec_w_in[j], lower_bounds[layer], rec_out_norm[j], rec_w_out[j])
        h = h + rms_norm(y, norm_post[layer])
        gate = jax.nn.sigmoid((h @ ple_w_gate[layer]).astype(jnp.float32)).astype(h.dtype)
        h = h + (p[layer] @ ple_w_proj[layer]) * gate
    return h
```

```python
from concourse.bass_utils import run_bass_kernel_spmd

import numpy as np
import concourse.bass as bass
import concourse.mybir as mybir
from contextlib import ExitStack

F32 = mybir.dt.float32
BF16 = mybir.dt.bfloat16
AF = mybir.ActivationFunctionType
ALU = mybir.AluOpType
AX = mybir.AxisListType

ENGS = ("pe", "act", "dve", "pool", "sp")
N_DMA_SEMS = 7
EPOCH = 30000


class _Op:
    __slots__ = ("eng", "fn", "deps", "signal", "sig", "is_dma", "prewait", "idx")


class Prog:
    def __init__(self, nc, es, bar=None, sem_es=None, tag=""):
        self.nc = nc
        self.es = es
        self.sem_es = sem_es if sem_es is not None else es
        self.tag = tag
        self.bar = bar
        self.ops = []
        self.last_w = {}
        self.readers = {}
        self.cap = None

    def op(self, eng, fn, reads=(), writes=(), dma=False):
        o = _Op()
        o.eng, o.fn, o.is_dma, o.signal, o.sig, o.prewait = eng, fn, dma, dma, None, None
        o.deps = (tuple(reads), tuple(writes))
        o.idx = -1
        if self.cap is not None:
            self.cap.append(o)
        else:
            self._record(o)
        return o

    def capture(self, gen):
        prev, self.cap = self.cap, []
        try:
            gen()
            return self.cap
        finally:
            self.cap = prev

    def replay(self, main, side=()):
        main, side = list(main), list(side)
        nm, ns = len(main), len(side)
        j = 0
        for i, o in enumerate(main):
            self._record(o)
            tgt = ((i + 1) * ns) // max(nm, 1)
            while j < tgt:
                self._record(side[j])
                j += 1
        while j < ns:
            self._record(side[j])
            j += 1

    def _record(self, o):
        reads, writes = o.deps
        o.idx = len(self.ops)
        deps = set()
        for r in reads:
            w = self.last_w.get(r)
            if w is not None:
                deps.add(w)
        for w_ in writes:
            w = self.last_w.get(w_)
            if w is not None:
                deps.add(w)
            for rd in self.readers.get(w_, ()):
                deps.add(rd)
        deps.discard(o)
        o.deps = deps
        for d in deps:
            if not (d.eng == "pe" and o.eng == "pe"):
                d.signal = True
        for r in reads:
            self.readers.setdefault(r, []).append(o)
        for w_ in writes:
            self.last_w[w_] = o
            self.readers[w_] = []
        self.ops.append(o)

    def emit(self, final_wait_ops=()):
        nc, es, tag = self.nc, self.sem_es, self.tag
        cnt = {e: 0 for e in ENGS}
        n_ep = {e: 0 for e in ENGS}
        for o in self.ops:
            if o.signal and not o.is_dma:
                cnt[o.eng] += 1
        eng_sems = {}
        for e in ENGS:
            n = (cnt[e] + EPOCH - 1) // EPOCH
            eng_sems[e] = [es.enter_context(nc.semaphore(f"s{tag}_{e}_{i}")) for i in range(max(n, 1))]
        dma_engs = sorted({o.eng for o in self.ops if o.is_dma})
        dma_sems = {e: [es.enter_context(nc.semaphore(f"s{tag}_dma_{e}_{i}")) for i in range(N_DMA_SEMS)] for e in dma_engs}
        dma_uses = {e: [0] * N_DMA_SEMS for e in dma_engs}
        dma_k = {e: 0 for e in dma_engs}
        cnt = {e: 0 for e in ENGS}
        for o in self.ops:
            if o.is_dma:
                k = dma_k[o.eng]
                sems, uses = dma_sems[o.eng], dma_uses[o.eng]
                if uses[k] > 0:
                    o.prewait = (sems[k], 16 * uses[k])
                uses[k] += 1
                o.sig = (sems[k], 16 * uses[k], 16)
                dma_k[o.eng] = (k + 1) % N_DMA_SEMS
            elif o.signal:
                c = cnt[o.eng]
                cnt[o.eng] += 1
                o.sig = (eng_sems[o.eng][c // EPOCH], (c % EPOCH) + 1, 1)
        per = {e: [o for o in self.ops if o.eng == e] for e in ENGS}
        final = list(final_wait_ops)

        bar = self.bar

        def run(e, h):
            seen = {}
            if bar is not None and bar[1] > 0:
                h.wait_ge(bar[0], 5 * bar[1])
            def wait(sem, val):
                key = id(sem)
                if seen.get(key, 0) < val:
                    h.wait_ge(sem, val)
                    seen[key] = val
            for o in per[e]:
                for d in sorted(o.deps, key=lambda d: d.idx):
                    if d.eng == "pe" and e == "pe":
                        continue
                    wait(d.sig[0], d.sig[1])
                if o.prewait is not None:
                    wait(*o.prewait)
                inst = o.fn(h)
                if o.sig is not None:
                    inst.then_inc(o.sig[0], o.sig[2])
            if e == "sp":
                for o in final:
                    wait(o.sig[0], o.sig[1])
            if bar is not None:
                h.drain().then_inc(bar[0], 1)

        with nc.Block() as block:
            @block.tensor
            def _(h):
                run("pe", h)

            @block.scalar
            def _(h):
                run("act", h)

            @block.vector
            def _(h):
                run("dve", h)

            @block.gpsimd
            def _(h):
                run("pool", h)

            @block.sync
            def _(h):
                run("sp", h)


EPS = 1e-6


def make_ident(P, ident):
    P.op("pool", lambda h: h.memset(ident[:], 1.0), writes=["ident"])
    P.op("pool", lambda h: h.affine_select(out=ident[:], in_=ident[:], pattern=[[1, 128]],
                                           compare_op=ALU.is_equal, fill=0.0, base=0,
                                           channel_multiplier=-1),
         reads=["ident"], writes=["ident"])


def _mk(nc, es):
    def sb(name, shape, dt):
        return es.enter_context(nc.sbuf_tensor(name, shape, dt))

    def ps(name, shape, dt):
        return es.enter_context(nc.psum_tensor(name, shape, dt))
    return sb, ps


def emit_post(P, nc, es, T, emit_u, aT_d, x_d, pT_d, wo_d, wg_d, wp_d, gpost_d, gpre_d, h_out, uT_out, pfx):
    sb, ps = _mk(nc, es)
    NT = T // 128
    ident = sb(pfx + "ident", [128, 128], BF16)
    mhalf = sb(pfx + "mhalf", [128, 1], F32)
    wstage = sb(pfx + "wstage", [128, 8, 1024], F32)
    wo_bf = sb(pfx + "wo_bf", [128, 8, 1024], BF16)
    wg_bf = sb(pfx + "wg_bf", [128, 8, 1024], BF16)
    wp_bf = sb(pfx + "wp_bf", [128, 2, 1024], BF16)
    gpost_b = sb(pfx + "gpost_b", [128, 1024], F32)
    gpre_b = sb(pfx + "gpre_b", [128, 1024], F32)
    aT_blk = [sb(pfx + f"aT_blk{i}", [128, 8, 512], BF16) for i in range(2)]
    pT_st = [sb(pfx + f"pT_st{i}", [128, 2, 512], F32) for i in range(2)]
    pT_bf = [sb(pfx + f"pT_bf{i}", [128, 2, 512], BF16) for i in range(2)]
    uT_blk = [sb(pfx + f"uT_blk{i}", [128, 8, 512], BF16) for i in range(2)]
    x_sb = [sb(pfx + f"x_sb{i}", [128, 1024], F32) for i in range(2)]
    h1 = [sb(pfx + f"h1_{i}", [128, 1024], F32) for i in range(2)]
    hn = [sb(pfx + f"hn_{i}", [128, 1024], F32) for i in range(2)]
    h1bf = [sb(pfx + f"h1bf{i}", [128, 1024], BF16) for i in range(2)]
    ysb = sb(pfx + "ysb", [128, 1024], F32)
    sq = sb(pfx + "sq", [128, 1024], BF16)
    tmp = sb(pfx + "tmp", [128, 1024], F32)
    tmp2 = sb(pfx + "tmp2", [128, 1024], F32)
    h1T = sb(pfx + "h1T", [128, 1024], BF16)
    ubf = sb(pfx + "ubf", [128, 1024], BF16)
    gate = sb(pfx + "gate", [128, 1024], F32)
    st = sb(pfx + "st", [128, 8], F32)
    py = ps(pfx + "py", [128, 1024], F32)
    pg = ps(pfx + "pg", [128, 1024], F32)
    pp = ps(pfx + "pp", [128, 1024], F32)
    ptr = ps(pfx + "ptr", [128, 1024], BF16)
    ptr2 = ps(pfx + "ptr2", [128, 1024], BF16)

    make_ident(P, ident)
    P.op("pool", lambda h: h.memset(mhalf[:], -0.5), writes=["mhalf"])
    P.op("sp", lambda h: h.dma_start(out=gpost_b[:], in_=gpost_d.partition_broadcast(128)), writes=["gpost_b"], dma=True)
    if emit_u:
        P.op("sp", lambda h: h.dma_start(out=gpre_b[:], in_=gpre_d.partition_broadcast(128)), writes=["gpre_b"], dma=True)
    for (wd, wb, nm, nk) in ((wo_d, wo_bf, "wo_bf", 8), (wg_d, wg_bf, "wg_bf", 8), (wp_d, wp_bf, "wp_bf", 2)):
        P.op("sp", lambda h, wd=wd, nk=nk: h.dma_start(out=wstage[:, 0:nk, :], in_=wd.rearrange("(k p) n -> p k n", p=128)),
             writes=["wstage"], dma=True)
        P.op("dve", lambda h, wb=wb, nk=nk: h.tensor_copy(out=wb[:], in_=wstage[:, 0:nk, :]), reads=["wstage"], writes=[nm])
    aT_v = aT_d.rearrange("(k p) t -> p k t", p=128)
    pT_v = pT_d.rearrange("(k p) t -> p k t", p=128)
    uT_v = uT_out.rearrange("(k p) t -> p k t", p=128) if emit_u else None
    out_ops = []

    def rms(src, rd, c0, nm):
        P.op("dve", lambda h: h.scalar_tensor_tensor(out=sq[:], in0=src[:], scalar=1.0, in1=src[:], op0=ALU.mult, op1=ALU.mult,
                                                     accum_out=st[:, c0:c0 + 1]), reads=[rd], writes=["sq", nm + "ss"])
        P.op("dve", lambda h: h.tensor_scalar(out=st[:, c0 + 1:c0 + 2], in0=st[:, c0:c0 + 1], scalar1=1.0 / 1024, scalar2=EPS,
                                              op0=ALU.mult, op1=ALU.add), reads=[nm + "ss"], writes=[nm + "rt"])
        P.op("pool", lambda h: h.tensor_tensor(out=st[:, c0 + 2:c0 + 3], in0=st[:, c0 + 1:c0 + 2], in1=mhalf[:], op=ALU.pow),
             reads=[nm + "rt", "mhalf"], writes=[nm + "rstd"])

    def stage_a(t):
        s, b, tt = t % 2, t // 4, t % 4
        bs = b % 2
        tok = slice(tt * 128, (tt + 1) * 128)
        if tt == 0:
            P.op("sp", lambda h: h.dma_start(out=aT_blk[bs][:], in_=aT_v[:, :, b * 512:(b + 1) * 512]), writes=[f"aT{bs}"], dma=True)
            P.op("sp", lambda h: h.dma_start(out=pT_st[bs][:], in_=pT_v[:, :, b * 512:(b + 1) * 512]), writes=[f"pTst{bs}"], dma=True)
            P.op("pool", lambda h: h.tensor_copy(out=pT_bf[bs][:], in_=pT_st[bs][:]), reads=[f"pTst{bs}"], writes=[f"pTbf{bs}"])
        P.op("sp", lambda h: h.dma_start(out=x_sb[s][:], in_=x_d[t * 128:(t + 1) * 128, :]), writes=[f"x{s}"], dma=True)
        for half in range(2):
            hs = slice(half * 512, (half + 1) * 512)
            for k in range(8):
                P.op("pe", lambda h, k=k, hs=hs: h.matmul(py[:, hs], lhsT=aT_blk[bs][:, k, tok], rhs=wo_bf[:, k, hs], start=(k == 0), stop=(k == 7)),
                     reads=[f"aT{bs}", "wo_bf"], writes=["py"])
        P.op("act", lambda h: h.copy(out=ysb[:], in_=py[:]), reads=["py"], writes=["ysb"])
        rms(ysb, "ysb", 0, "a")
        P.op("dve", lambda h: h.scalar_tensor_tensor(out=tmp[:], in0=ysb[:], scalar=st[:, 2:3], in1=gpost_b[:], op0=ALU.mult, op1=ALU.mult),
             reads=["ysb", "arstd", "gpost_b"], writes=["tmp"])
        P.op("dve", lambda h: h.tensor_tensor(out=h1[s][:], in0=tmp[:], in1=x_sb[s][:], op=ALU.add), reads=["tmp", f"x{s}"], writes=[f"h1{s}"])
        P.op("act", lambda h: h.copy(out=h1bf[s][:], in_=h1[s][:]), reads=[f"h1{s}"], writes=[f"h1bf{s}"])

    def stage_b(t):
        s, b, tt = t % 2, t // 4, t % 4
        bs = b % 2
        tok = slice(tt * 128, (tt + 1) * 128)
        for k in range(8):
            ks = slice(k * 128, (k + 1) * 128)
            P.op("pe", lambda h, ks=ks: h.transpose(out=ptr[:, ks], in_=h1bf[s][:, ks], identity=ident[:]), reads=[f"h1bf{s}", "ident"], writes=["ptr"])
        P.op("dve", lambda h: h.tensor_copy(out=h1T[:], in_=ptr[:]), reads=["ptr"], writes=["h1T"])
        for half in range(2):
            hs = slice(half * 512, (half + 1) * 512)
            for k in range(8):
                ks = slice(k * 128, (k + 1) * 128)
                P.op("pe", lambda h, k=k, ks=ks, hs=hs: h.matmul(pg[:, hs], lhsT=h1T[:, ks], rhs=wg_bf[:, k, hs], start=(k == 0), stop=(k == 7)),
                     reads=["h1T", "wg_bf"], writes=["pg"])
        P.op("act", lambda h: h.activation(out=gate[:], in_=pg[:], func=AF.Sigmoid), reads=["pg"], writes=["gate"])
        for half in range(2):
            hs = slice(half * 512, (half + 1) * 512)
            for k in range(2):
                P.op("pe", lambda h, k=k, hs=hs: h.matmul(pp[:, hs], lhsT=pT_bf[bs][:, k, tok], rhs=wp_bf[:, k, hs], start=(k == 0), stop=(k == 1)),
                     reads=[f"pTbf{bs}", "wp_bf"], writes=["pp"])
        P.op("dve", lambda h: h.tensor_tensor(out=tmp2[:], in0=pp[:], in1=gate[:], op=ALU.mult), reads=["pp", "gate"], writes=["tmp2"])
        P.op("dve", lambda h: h.tensor_tensor(out=hn[s][:], in0=h1[s][:], in1=tmp2[:], op=ALU.add), reads=[f"h1{s}", "tmp2"], writes=[f"hn{s}"])
        o = P.op("pool", lambda h: h.dma_start(out=h_out[t * 128:(t + 1) * 128, :], in_=hn[s][:]), reads=[f"hn{s}"], dma=True)
        out_ops.append(o)
        if emit_u:
            rms(hn[s], f"hn{s}", 3, "c")
            P.op("dve", lambda h: h.scalar_tensor_tensor(out=ubf[:], in0=hn[s][:], scalar=st[:, 5:6], in1=gpre_b[:], op0=ALU.mult, op1=ALU.mult),
                 reads=[f"hn{s}", "crstd", "gpre_b"], writes=["ubf"])
            for k in range(8):
                ks = slice(k * 128, (k + 1) * 128)
                P.op("pe", lambda h, ks=ks: h.transpose(out=ptr2[:, ks], in_=ubf[:, ks], identity=ident[:]), reads=["ubf", "ident"], writes=["ptr2"])
            P.op("act", lambda h: h.copy(out=uT_blk[bs][:, :, tok], in_=ptr2[:].rearrange("p (k t) -> p k t", k=8)), reads=["ptr2"], writes=[f"uT{bs}"])
            if tt == 3:
                o = P.op("pool", lambda h: h.dma_start(out=uT_v[:, :, b * 512:(b + 1) * 512], in_=uT_blk[bs][:]), reads=[f"uT{bs}"], dma=True)
                out_ops.append(o)

    P.replay(P.capture(lambda: stage_a(0)))
    for t in range(NT):
        main = P.capture(lambda: stage_b(t))
        side = P.capture(lambda: stage_a(t + 1)) if t + 1 < NT else []
        P.replay(main, side)
    return out_ops


def emit_att(P, nc, es, S, n_pairs, x_d, w_d, gp_d, bf_d, aT_out, cscr, pfx, ucache=None):
    sb, ps = _mk(nc, es)
    NB = S // 512
    NKT = S // 128
    QB, KB, VB, GB, FB = 0, n_pairs * 128, 2 * n_pairs * 128, 3 * n_pairs * 128, 4 * n_pairs * 128
    ident = sb(pfx + "ident", [128, 128], BF16)
    mneg = sb(pfx + "mneg", [128, 128], F32)
    ones_f = sb(pfx + "ones_f", [128, 64], F32)
    ones2 = sb(pfx + "ones2", [2, 512], F32)
    gp = sb(pfx + "gp", [128, 8], F32)
    wst = sb(pfx + "wst", [128, 8, 514], F32)
    wsc = sb(pfx + "wsc", [128, 8, 514], BF16)
    KT = [sb(pfx + f"KT{h}", [70, S], BF16) for h in range(2)]
    Vt = sb(pfx + "Vt", [128, NKT, 2, 65], BF16)
    Qaug = [[sb(pfx + f"Qaug{h}_{s}", [70, 512], BF16) for s in range(2)] for h in range(2)]
    sgf = [sb(pfx + f"sgf{s}", [128, 512], BF16) for s in range(2)]
    sgB = [sb(pfx + f"sgB{s}", [64, 512], BF16) for s in range(2)]
    sg = [[sgf[s][0:64, :] for s in range(2)], [sgB[s][:] for s in range(2)]]
    qst = sb(pfx + "qst", [128, 512], BF16)
    kst = sb(pfx + "kst", [128, 512], BF16)
    x_sb = [sb(pfx + f"x_sb{i}", [128, 1024], F32) for i in range(2)]
    sq = sb(pfx + "sq", [128, 1024], BF16)
    usc = [sb(pfx + f"usc{i}", [128, 1024], BF16) for i in range(4)]
    uT = [sb(pfx + f"uT{i}", [128, 8, 512], BF16) for i in range(2)]
    st = sb(pfx + "st", [128, 16], F32)
    mhalf = sb(pfx + "mhalf", [128, 1], F32)
    eg = sb(pfx + "eg", [128, 512], F32)
    eg2 = sb(pfx + "eg2", [128, 512], F32)
    bt = sb(pfx + "bt", [2, 2], F32)
    e1 = sb(pfx + "e1", [2, 512], F32)
    spl = sb(pfx + "spl", [2, 512], F32)
    cpos = [sb(pfx + f"cpos{i}", [2, 512], F32) for i in range(2)]
    r1 = sb(pfx + "r1", [2, 512], F32)
    r2 = sb(pfx + "r2", [2, 512], F32)
    ST = sb(pfx + "ST", [2, 6, 512], BF16)
    Pt = [sb(pfx + f"Pt{i}", [128, 512], BF16) for i in range(3)]
    rden = sb(pfx + "rden", [128, 512], F32)
    tnorm = sb(pfx + "tnorm", [64, 512], F32)
    a_blk = [[sb(pfx + f"a_blk{h}_{s}", [64, 512], BF16) for s in range(2)] for h in range(2)]
    ptr = ps(pfx + "ptr", [128, 1024], BF16)
    pj = [ps(pfx + f"pj{i}", [128, 512], F32) for i in range(2)]
    ps_s = [ps(pfx + f"ps_s{i}", [128, 512], F32) for i in range(3)]
    po = [ps(pfx + f"po{i}", [128, 512], F32) for i in range(2)]

    make_ident(P, ident)
    P.op("pool", lambda h: h.memset(mneg[:], 3.0e38), writes=["mneg"])
    P.op("pool", lambda h: h.affine_select(out=mneg[:], in_=mneg[:], pattern=[[1, 128]], compare_op=ALU.is_ge, fill=-30000.0,
                                           base=0, channel_multiplier=-1), reads=["mneg"], writes=["mneg"])
    P.op("pool", lambda h: h.memset(ones_f[:], 1.0), writes=["ones_f"])
    P.op("pool", lambda h: h.memset(ones2[:], 1.0), writes=["ones2"])
    P.op("pool", lambda h: h.memset(mhalf[:], -0.5), writes=["mhalf"])
    P.op("sp", lambda h: h.dma_start(out=gp[:], in_=gp_d), writes=["gp"], dma=True)
    w_v = w_d.rearrange("(k p) n -> p k n", p=128)
    uc_v = ucache.rearrange("(k p) t -> p k t", p=128) if ucache is not None else None
    out_ops = []
    for pp in range(n_pairs):
        for gi_, base in enumerate((QB, KB, VB, GB)):
            P.op("sp", lambda h, gi_=gi_, base=base, pp=pp: h.dma_start(
                out=wst[:, :, gi_ * 128:(gi_ + 1) * 128], in_=w_v[:, :, base + pp * 128:base + (pp + 1) * 128]),
                writes=["wst"], dma=True)
        P.op("sp", lambda h, pp=pp: h.dma_start(out=wst[:, :, 512:514], in_=w_v[:, :, FB + 2 * pp:FB + 2 * pp + 2]),
             writes=["wst"], dma=True)
        for k in range(8):
            P.op("dve", lambda h, k=k: h.tensor_scalar(out=wsc[:, k, :], in0=wst[:, k, :], scalar1=gp[:, k:k + 1], scalar2=None, op0=ALU.mult),
                 reads=["wst", "gp"], writes=["wsc"])
        P.op("sp", lambda h, pp=pp: h.dma_start(out=bt[:, 0:1], in_=bf_d[pp]), writes=["bt"], dma=True)
        P.op("dve", lambda h: h.tensor_scalar(out=bt[:, 1:2], in0=bt[:, 0:1], scalar1=-1.0, scalar2=None, op0=ALU.mult),
             reads=["bt"], writes=["negb"])
        P.op("pool", lambda h: h.memset(Vt[:], 1.0), writes=[f"V_{kb}" for kb in range(NB)])
        for hh in range(2):
            P.op("pool", lambda h, hh=hh: h.memset(KT[hh][64:70, :], 1.0), writes=[f"KTaug{hh}_{kb}" for kb in range(NB)])
            for s in range(2):
                P.op("pool", lambda h, hh=hh, s=s: h.memset(Qaug[hh][s][64:70, :], 1.0), writes=[f"Qaug_aug{hh}_{s}"])
        def prep(jb):
            s2 = jb % 2
            blk = slice(jb * 512, (jb + 1) * 512)
            if ucache is not None and pp > 0:
                P.op("sp", lambda h, s2=s2, blk=blk: h.dma_start(out=uT[s2][:], in_=uc_v[:, :, blk]), reads=[f"ucache{jb}"], writes=[f"uT{s2}"], dma=True)
            else:
                for tt in range(4):
                    t = jb * 4 + tt
                    s = t % 2
                    c = 4 * tt
                    P.op("sp", lambda h, t=t, s=s: h.dma_start(out=x_sb[s][:], in_=x_d[t * 128:(t + 1) * 128, :]), writes=[f"x{s}"], dma=True)
                    P.op("dve", lambda h, s=s, c=c: h.scalar_tensor_tensor(out=sq[:], in0=x_sb[s][:], scalar=1.0, in1=x_sb[s][:],
                                                                           op0=ALU.mult, op1=ALU.mult, accum_out=st[:, c:c + 1]),
                         reads=[f"x{s}"], writes=["sq", f"ss{tt}"])
                    P.op("dve", lambda h, c=c: h.tensor_scalar(out=st[:, c + 1:c + 2], in0=st[:, c:c + 1], scalar1=1.0 / 1024, scalar2=EPS,
                                                               op0=ALU.mult, op1=ALU.add), reads=[f"ss{tt}"], writes=[f"rt{tt}"])
                    P.op("pool", lambda h, c=c: h.tensor_tensor(out=st[:, c + 2:c + 3], in0=st[:, c + 1:c + 2], in1=mhalf[:], op=ALU.pow),
                         reads=[f"rt{tt}", "mhalf"], writes=[f"rstd{tt}"])
                    P.op("dve", lambda h, s=s, c=c, tt=tt: h.tensor_scalar(out=usc[tt][:], in0=x_sb[s][:], scalar1=st[:, c + 2:c + 3], scalar2=None, op0=ALU.mult),
                         reads=[f"x{s}", f"rstd{tt}"], writes=[f"usc{tt}"])
                for tt in range(4):
                    tok = slice(tt * 128, (tt + 1) * 128)
                    for k in range(8):
                        ks = slice(k * 128, (k + 1) * 128)
                        P.op("pe", lambda h, ks=ks, tt=tt: h.transpose(out=ptr[:, ks], in_=usc[tt][:, ks], identity=ident[:]),
                             reads=[f"usc{tt}", "ident"], writes=["ptr"])
                    P.op("dve", lambda h, s2=s2, tok=tok: h.tensor_copy(out=uT[s2][:, :, tok], in_=ptr[:].rearrange("p (k t) -> p k t", k=8)),
                         reads=["ptr"], writes=[f"uT{s2}"])
                if ucache is not None:
                    P.op("sp", lambda h, s2=s2, blk=blk: h.dma_start(out=uc_v[:, :, blk], in_=uT[s2][:]), reads=[f"uT{s2}"], writes=[f"ucache{jb}"], dma=True)
            gi = 0
            for kind in ("q", "k", "g"):
                base = {"q": 0, "k": 128, "g": 384}[kind]
                pjb = pj[gi % 2]
                pjn = f"pj{gi % 2}"
                gi += 1
                for k in range(8):
                    P.op("pe", lambda h, k=k, base=base, pjb=pjb, s2=s2: h.matmul(
                        pjb[:, :], lhsT=wsc[:, k, base:base + 128], rhs=uT[s2][:, k, :], start=(k == 0), stop=(k == 7)),
                        reads=["wsc", f"uT{s2}"], writes=[pjn])
                if kind == "q":
                    P.op("dve", lambda h, pjb=pjb, s2=s2: h.tensor_scalar(out=Qaug[0][s2][0:64, :], in0=pjb[0:64, :], scalar1=0.125, scalar2=None, op0=ALU.mult),
                         reads=[pjn], writes=[f"Qaug_m0_{s2}"])
                    P.op("dve", lambda h, pjb=pjb: h.tensor_scalar(out=qst[64:128, :], in0=pjb[64:128, :], scalar1=0.125, scalar2=None, op0=ALU.mult),
                         reads=[pjn], writes=["qst"])
                    P.op("pool", lambda h, s2=s2: h.dma_start(out=Qaug[1][s2][0:64, :], in_=qst[64:128, :]), reads=["qst"], writes=[f"Qaug_m1_{s2}"], dma=True)
                elif kind == "k":
                    P.op("dve", lambda h, pjb=pjb, blk=blk: h.tensor_copy(out=KT[0][0:64, blk], in_=pjb[0:64, :]), reads=[pjn], writes=[f"KT0_{jb}"])
                    P.op("dve", lambda h, pjb=pjb: h.tensor_copy(out=kst[64:128, :], in_=pjb[64:128, :]), reads=[pjn], writes=["kst"])
                    P.op("pool", lambda h, blk=blk: h.dma_start(out=KT[1][0:64, blk], in_=kst[64:128, :]), reads=["kst"], writes=[f"KT1_{jb}"], dma=True)
                else:
                    P.op("act", lambda h, pjb=pjb: h.activation(out=eg[:], in_=pjb[:, :], func=AF.Exp, scale=-1.0), reads=[pjn], writes=["eg"])
                    P.op("act", lambda h: h.activation(out=eg2[:], in_=eg[:], func=AF.Ln, scale=1.0, bias=1.0), reads=["eg"], writes=["eg2"])
                    P.op("act", lambda h: h.activation(out=eg[:], in_=eg2[:], func=AF.Exp, scale=-1.0), reads=["eg2"], writes=["eg"])
                    P.op("dve", lambda h, pjb=pjb, s2=s2: h.tensor_tensor(out=sgf[s2][:], in0=pjb[:, :], in1=eg[:], op=ALU.mult),
                         reads=[pjn, "eg"], writes=[f"sg0_{s2}", f"sgf_{s2}"])
                    P.op("pool", lambda h, s2=s2: h.dma_start(out=sg[1][s2][:], in_=sgf[s2][64:128, :]), reads=[f"sgf_{s2}"], writes=[f"sg1_{s2}"], dma=True)
            pjb = pj[gi % 2]
            pjn = f"pj{gi % 2}"
            gi += 1
            for tt in range(4):
                tok = slice(tt * 128, (tt + 1) * 128)
                for k in range(8):
                    P.op("pe", lambda h, k=k, pjb=pjb, s2=s2, tok=tok: h.matmul(
                        pjb[:, tok], lhsT=uT[s2][:, k, tok], rhs=wsc[:, k, 256:384], start=(k == 0), stop=(k == 7)),
                        reads=["wsc", f"uT{s2}"], writes=[pjn])
            P.op("dve", lambda h, pjb=pjb, jb=jb: h.tensor_copy(out=Vt[:, jb * 4:(jb + 1) * 4, :, 0:64],
                                                                 in_=pjb[:].rearrange("p (t h d) -> p t h d", t=4, h=2)),
                 reads=[pjn], writes=[f"V_{jb}"])
            pjb = pj[gi % 2]
            pjn = f"pj{gi % 2}"
            gi += 1
            for k in range(8):
                P.op("pe", lambda h, k=k, pjb=pjb, s2=s2: h.matmul(
                    pjb[0:2, :], lhsT=wsc[:, k, 512:514], rhs=uT[s2][:, k, :], start=(k == 0), stop=(k == 7)),
                    reads=["wsc", f"uT{s2}"], writes=[pjn])
            P.op("act", lambda h, pjb=pjb: h.activation(out=e1[:], in_=pjb[0:2, :], func=AF.Exp, scale=-1.0, bias=bt[:, 1:2]),
                 reads=[pjn, "negb"], writes=["e1"])
            P.op("act", lambda h: h.activation(out=spl[:], in_=e1[:], func=AF.Ln, scale=1.0, bias=1.0), reads=["e1"], writes=["spl"])
            init = 0.0 if jb == 0 else cpos[1 - s2][:, 511:512]
            P.op("dve", lambda h, s2=s2, init=init: h.tensor_tensor_scan(out=cpos[s2][:], data0=ones2[:], data1=spl[:], initial=init,
                                                                         op0=ALU.mult, op1=ALU.add),
                 reads=["ones2", "spl", f"cpos{1 - s2}"], writes=[f"cpos{s2}"])
            P.op("dve", lambda h, s2=s2: h.tensor_copy(out=ST[:, 3, :], in_=cpos[s2][:]), reads=[f"cpos{s2}"], writes=["ST3"])
            P.op("dve", lambda h, s2=s2: h.tensor_tensor(out=r1[:], in0=cpos[s2][:], in1=ST[:, 3, :], op=ALU.subtract), reads=[f"cpos{s2}", "ST3"], writes=["r1"])
            P.op("dve", lambda h: h.tensor_copy(out=ST[:, 4, :], in_=r1[:]), reads=["r1"], writes=["ST4"])
            P.op("dve", lambda h: h.tensor_tensor(out=r2[:], in0=r1[:], in1=ST[:, 4, :], op=ALU.subtract), reads=["r1", "ST4"], writes=["r2"])
            P.op("dve", lambda h: h.tensor_copy(out=ST[:, 5, :], in_=r2[:]), reads=["r2"], writes=["ST5"])
            P.op("dve", lambda h: h.tensor_scalar(out=ST[:, 0:3, :], in0=ST[:, 3:6, :], scalar1=-1.0, scalar2=None, op0=ALU.mult),
                 reads=["ST3", "ST4", "ST5"], writes=["ST0"])
            ci = pp * NB + jb
            P.op("pool", lambda h, ci=ci: h.dma_start(out=cscr[ci], in_=ST[:]), reads=["ST0", "ST3", "ST4", "ST5"], writes=[f"cscr{ci}"], dma=True)
            for hh in range(2):
                P.op("pool", lambda h, ci=ci, hh=hh, s2=s2: h.dma_start(out=Qaug[hh][s2][64:67, :], in_=cscr[ci, hh, 0:3, :]),
                     reads=[f"cscr{ci}"], writes=[f"Qaug_aug{hh}_{s2}"], dma=True)
                P.op("pool", lambda h, ci=ci, hh=hh, blk=blk: h.dma_start(out=KT[hh][67:70, blk], in_=cscr[ci, hh, 3:6, :]),
                     reads=[f"cscr{ci}"], writes=[f"KTaug{hh}_{jb}"], dma=True)

        def attend(jb):
            s2 = jb % 2
            nkt = 4 * (jb + 1)
            items = [(hh, kt) for kt in range(nkt) for hh in range(2)]
            n = len(items)
            for i in range(n + 2):
                if i < n:
                    hh, kt = items[i]
                    r = kt - 4 * jb
                    c0 = max(r, 0) * 128
                    kb = kt // 4
                    sl = i % 3
                    P.op("pe", lambda h, hh=hh, kt=kt, c0=c0, sl=sl, s2=s2: h.matmul(
                        ps_s[sl][:, c0:512], lhsT=KT[hh][0:70, kt * 128:(kt + 1) * 128], rhs=Qaug[hh][s2][0:70, c0:512], start=True, stop=True),
                        reads=[f"KT{hh}_{kb}", f"KTaug{hh}_{kb}", f"Qaug_m{hh}_{s2}", f"Qaug_aug{hh}_{s2}"], writes=[f"ps_s{sl}"])
                    if r >= 0:
                        P.op("dve", lambda h, sl=sl, c0=c0: h.tensor_tensor(out=ps_s[sl][:, c0:c0 + 128], in0=ps_s[sl][:, c0:c0 + 128], in1=mneg[:], op=ALU.min),
                             reads=[f"ps_s{sl}", "mneg"], writes=[f"ps_s{sl}"])
                j = i - 2
                if j >= 0:
                    hh, kt = items[j]
                    r = kt - 4 * jb
                    c0 = max(r, 0) * 128
                    kb = kt // 4
                    sl = j % 3
                    P.op("act", lambda h, sl=sl, c0=c0: h.activation(out=Pt[sl][:, c0:512], in_=ps_s[sl][:, c0:512], func=AF.Exp),
                         reads=[f"ps_s{sl}"], writes=[f"Pt{sl}"])
                    P.op("pe", lambda h, hh=hh, kt=kt, c0=c0, sl=sl, nkt=nkt: h.matmul(
                        po[hh][0:65, c0:512], lhsT=Vt[:, kt, hh, 0:65], rhs=Pt[sl][:, c0:512], start=(kt == 0), stop=(kt == nkt - 1)),
                        reads=[f"V_{kb}", f"Pt{sl}"], writes=[f"po{hh}"])

        def finish(jb):
            s2 = jb % 2
            blk = slice(jb * 512, (jb + 1) * 512)
            for hh in range(2):
                P.op("act", lambda h, hh=hh: h.activation(out=rden[64:65, :], in_=po[hh][64:65, :], func=AF.Ln), reads=[f"po{hh}"], writes=["rden"])
                P.op("act", lambda h: h.activation(out=rden[64:65, :], in_=rden[64:65, :], func=AF.Exp, scale=-1.0), reads=["rden"], writes=["rden"])
                P.op("pe", lambda h: h.matmul(pj[0][0:64, :], lhsT=ones_f[64:65, 0:64], rhs=rden[64:65, :], start=True, stop=True),
                     reads=["ones_f", "rden"], writes=["pj0"])
                P.op("dve", lambda h, hh=hh, s2=s2: h.tensor_tensor(out=tnorm[:], in0=pj[0][0:64, :], in1=sg[hh][s2], op=ALU.mult),
                     reads=["pj0", f"sg{hh}_{s2}"], writes=["tnorm"])
                P.op("dve", lambda h, hh=hh, s2=s2: h.tensor_tensor(out=a_blk[hh][s2][:], in0=po[hh][0:64, :], in1=tnorm[:], op=ALU.mult),
                     reads=[f"po{hh}", "tnorm"], writes=[f"a_blk{hh}_{s2}"])
                row = (2 * pp + hh) * 64
                o = P.op("pool", lambda h, hh=hh, s2=s2, row=row, blk=blk: h.dma_start(out=aT_out[row:row + 64, blk], in_=a_blk[hh][s2][:]),
                         reads=[f"a_blk{hh}_{s2}"], dma=True)
                out_ops.append(o)

        P.replay(P.capture(lambda: prep(0)))
        for jb in range(NB):
            main = P.capture(lambda: attend(jb))
            side = P.capture(lambda: prep(jb + 1)) if jb + 1 < NB else []
            P.replay(main, side)
            finish(jb)
    return out_ops


def emit_rec(P, nc, es, S, n_pairs, uT_d, w_d, lb_d, og_d, yT_out, pfx):
    sb, ps = _mk(nc, es)
    NB = S // 512
    NH = 2 * n_pairs
    W = NH * 128
    ident = sb(pfx + "ident", [128, 128], BF16)
    tri = sb(pfx + "tri", [128, 64], F32)
    rmask = sb(pfx + "rmask", [128, 512], F32)
    mhalf = sb(pfx + "mhalf", [128, 1], F32)
    wst = sb(pfx + "wst", [128, 8, 256], F32)
    wbf = sb(pfx + "wbf", [128, 8, 1024], BF16)
    lbt = sb(pfx + "lbt", [128, 2, NH], F32)
    lbv = sb(pfx + "lbv", [128, 4, NH], F32)
    og_b = sb(pfx + "og_b", [128, 128], F32)
    uTb = [sb(pfx + f"uTb{i}", [128, 8, 512], BF16) for i in range(2)]
    sigT = sb(pfx + "sigT", [128, 512], F32)
    sgt1 = sb(pfx + "esgT", [128, 512], F32)
    fT = sb(pfx + "fT", [128, 512], F32)
    lfT = sb(pfx + "lfT", [128, 512], F32)
    bT = sb(pfx + "bT", [128, 512], F32)
    enbT = sb(pfx + "enbT", [128, 512], F32)
    edT = sb(pfx + "edT", [128, 512], F32)
    kT = sb(pfx + "kT", [128, 512], F32)
    KdT = sb(pfx + "KdT", [128, 512], BF16)
    ebT = [[sb(pfx + f"ebT{h}_{s}", [128, 512], F32) for s in range(2)] for h in range(2)]
    AT = [[sb(pfx + f"AT{h}_{s}", [128, 512], BF16) for s in range(2)] for h in range(2)]
    BT = [[sb(pfx + f"BT{h}_{s}", [128, 512], BF16) for s in range(2)] for h in range(2)]
    Kd = [[sb(pfx + f"Kd{h}_{s}", [128, 4, 128], BF16) for s in range(2)] for h in range(2)]
    inp_sb = [sb(pfx + f"inp_sb{s}", [128, 4, 256], BF16) for s in range(2)]
    sgt = [sb(pfx + f"sgt{s}", [128, 4, 256], BF16) for s in range(2)]
    eg = sb(pfx + "eg", [128, 256], F32)
    eg2 = sb(pfx + "eg2", [128, 256], F32)
    scT = [sb(pfx + f"scT{h}", [128, 64], BF16) for h in range(2)]
    stf = [sb(pfx + f"stf{h}", [128, 128], F32) for h in range(2)]
    stb = [[sb(pfx + f"stb{h}_{s}", [128, 128], BF16) for s in range(2)] for h in range(2)]
    osb = sb(pfx + "osb", [128, 128], F32)
    sq = sb(pfx + "sq", [128, 128], BF16)
    st = sb(pfx + "st", [128, 4], F32)
    t1 = sb(pfx + "t1", [128, 128], F32)
    ybf = sb(pfx + "ybf", [128, 128], BF16)
    yT_blk = [[sb(pfx + f"yT_blk{h}_{s}", [128, 512], BF16) for s in range(2)] for h in range(2)]
    pjf = ps(pfx + "pjf", [128, 512], F32)
    pjt = ps(pfx + "pjt", [128, 512], F32)
    psc = ps(pfx + "psc", [128, 512], F32)
    po = [ps(pfx + f"po{i}", [128, 512], F32) for i in range(2)]
    pst = ps(pfx + "pst", [128, 512], F32)
    ptr = ps(pfx + "ptr", [128, 1024], BF16)
    pty = ps(pfx + "pty", [128, 1024], BF16)

    make_ident(P, ident)
    P.op("pool", lambda h: h.memset(tri[:], 1.0), writes=["tri"])
    for half in range(2):
        P.op("pool", lambda h, half=half: h.affine_select(
            out=tri[half * 64:(half + 1) * 64, :], in_=tri[half * 64:(half + 1) * 64, :], pattern=[[1, 64]],
            compare_op=ALU.is_ge, fill=0.0, base=0, channel_multiplier=-1), reads=["tri"], writes=["tri"])
    P.op("pool", lambda h: h.memset(rmask[:], 1.0), writes=["rmask"])
    P.op("pool", lambda h: h.memset(rmask[:].rearrange("p (c t) -> p c t", t=64)[:, :, 0:1], 0.0), reads=["rmask"], writes=["rmask"])
    P.op("pool", lambda h: h.memset(mhalf[:], -0.5), writes=["mhalf"])
    P.op("sp", lambda h: h.dma_start(out=og_b[:], in_=og_d.partition_broadcast(128)), writes=["og_b"], dma=True)
    P.op("sp", lambda h: h.dma_start(out=lbt[:], in_=lb_d), writes=["lbt"], dma=True)
    P.op("dve", lambda h: h.tensor_tensor(out=lbv[:, 0, :], in0=lbt[:, 0, :], in1=lbt[:, 1, :], op=ALU.subtract), reads=["lbt"], writes=["lbdiff"])
    P.op("act", lambda h: h.activation(out=lbv[:, 0, :], in_=lbv[:, 0, :], func=AF.Exp), reads=["lbdiff"], writes=["lbdiff"])
    P.op("act", lambda h: h.activation(out=lbv[:, 0, :], in_=lbv[:, 0, :], func=AF.Ln, scale=1.0, bias=1.0), reads=["lbdiff"], writes=["lbdiff"])
    P.op("act", lambda h: h.activation(out=lbv[:, 1, :], in_=lbv[:, 0, :], func=AF.Exp, scale=-1.0), reads=["lbdiff"], writes=["lb"])
    P.op("dve", lambda h: h.tensor_scalar(out=lbv[:, 2, :], in0=lbv[:, 1, :], scalar1=-1.0, scalar2=1.0, op0=ALU.mult, op1=ALU.add),
         reads=["lb"], writes=["oml"])
    P.op("dve", lambda h: h.tensor_scalar(out=lbv[:, 3, :], in0=lbv[:, 1, :], scalar1=-1.0, scalar2=None, op0=ALU.add), reads=["lb"], writes=["noml"])
    w_v = w_d.rearrange("(k p) n -> p k n", p=128)
    uT_v = uT_d.rearrange("(k p) t -> p k t", p=128)
    out_ops = []
    for rp in range(n_pairs):
        for g in range(4):
            P.op("sp", lambda h, g=g, rp=rp: h.dma_start(out=wst[:], in_=w_v[:, :, g * W + rp * 256:g * W + (rp + 1) * 256]), writes=["wst"], dma=True)
            P.op("dve", lambda h, g=g: h.tensor_copy(out=wbf[:, :, g * 256:(g + 1) * 256], in_=wst[:]), reads=["wst"], writes=["wbf"])
        for hh in range(2):
            P.op("pool", lambda h, hh=hh: h.memset(stf[hh][:], 0.0), writes=[f"stf{hh}"])
            P.op("pool", lambda h, hh=hh: h.memset(stb[hh][0][:], 0.0), writes=[f"stb{hh}_0"])

        def prep(jb):
            s2 = jb % 2
            blk = slice(jb * 512, (jb + 1) * 512)
            P.op("sp", lambda h: h.dma_start(out=uTb[s2][:], in_=uT_v[:, :, blk]), writes=[f"uTb{s2}"], dma=True)
            for tt in range(4):
                tok = slice(tt * 128, (tt + 1) * 128)
                for k in range(8):
                    P.op("pe", lambda h, k=k, tok=tok: h.matmul(pjt[:], lhsT=uTb[s2][:, k, tok], rhs=wbf[:, k, 512:1024], start=(k == 0), stop=(k == 7)),
                         reads=["wbf", f"uTb{s2}"], writes=["pjt"])
                P.op("dve", lambda h, tt=tt: h.tensor_copy(out=inp_sb[s2][:, tt, :], in_=pjt[:, 0:256]), writes=["pjt", f"inp{s2}"])
                P.op("act", lambda h: h.activation(out=eg[:], in_=pjt[:, 256:512], func=AF.Exp, scale=-1.0), writes=["pjt", "eg"])
                P.op("act", lambda h: h.activation(out=eg2[:], in_=eg[:], func=AF.Ln, scale=1.0, bias=1.0), reads=["eg"], writes=["eg2"])
                P.op("act", lambda h: h.activation(out=eg[:], in_=eg2[:], func=AF.Exp, scale=-1.0), reads=["eg2"], writes=["eg"])
                P.op("dve", lambda h, tt=tt: h.tensor_tensor(out=sgt[s2][:, tt, :], in0=pjt[:, 256:512], in1=eg[:], op=ALU.mult),
                     reads=["eg"], writes=["pjt", f"sgt{s2}"])
            for hh in range(2):
                hd = 2 * rp + hh
                lb_c, oml_c, noml_c = lbv[:, 1, hd:hd + 1], lbv[:, 2, hd:hd + 1], lbv[:, 3, hd:hd + 1]
                fcol = 256 + hh * 128
                qcol = hh * 128
                for k in range(8):
                    P.op("pe", lambda h, k=k, fcol=fcol: h.matmul(pjf[:], lhsT=wbf[:, k, fcol:fcol + 128], rhs=uTb[s2][:, k, :], start=(k == 0), stop=(k == 7)),
                         reads=["wbf", f"uTb{s2}"], writes=["pjf"])
                P.op("act", lambda h: h.activation(out=sgt1[:], in_=pjf[:], func=AF.Exp, scale=-1.0), writes=["pjf", "sgt1"])
                for k in range(8):
                    P.op("pe", lambda h, k=k, qcol=qcol: h.matmul(pjf[:], lhsT=wbf[:, k, qcol:qcol + 128], rhs=uTb[s2][:, k, :], start=(k == 0), stop=(k == 7)),
                         reads=["wbf", f"uTb{s2}"], writes=["pjf"])
                P.op("act", lambda h: h.activation(out=sigT[:], in_=sgt1[:], func=AF.Ln, scale=1.0, bias=1.0), reads=["sgt1"], writes=["sigT"])
                P.op("act", lambda h: h.activation(out=sigT[:], in_=sigT[:], func=AF.Exp, scale=-1.0), reads=["sigT"], writes=["sigT"])
                P.op("dve", lambda h, oml_c=oml_c, lb_c=lb_c: h.tensor_scalar(out=fT[:], in0=sigT[:], scalar1=oml_c, scalar2=lb_c, op0=ALU.mult, op1=ALU.add),
                     reads=["sigT", "oml", "lb"], writes=["fT"])
                P.op("act", lambda h: h.activation(out=lfT[:], in_=fT[:], func=AF.Ln), reads=["fT"], writes=["lfT"])
                P.op("dve", lambda h: h.tensor_tensor_scan(out=bT[:], data0=rmask[:], data1=lfT[:], initial=0.0, op0=ALU.mult, op1=ALU.add),
                     reads=["rmask", "lfT"], writes=["bT"])
                P.op("act", lambda h, hh=hh: h.activation(out=ebT[hh][s2][:], in_=bT[:], func=AF.Exp), reads=["bT"], writes=[f"ebT{hh}_{s2}"])
                P.op("act", lambda h: h.activation(out=enbT[:], in_=bT[:], func=AF.Exp, scale=-1.0), reads=["bT"], writes=["enbT"])
                P.op("dve", lambda h, hh=hh: h.tensor_tensor(out=AT[hh][s2][:], in0=pjf[:], in1=ebT[hh][s2][:], op=ALU.mult),
                     reads=[f"ebT{hh}_{s2}"], writes=["pjf", f"AT{hh}_{s2}"])
                P.op("dve", lambda h, oml_c=oml_c, noml_c=noml_c: h.tensor_scalar(out=kT[:], in0=sigT[:], scalar1=noml_c, scalar2=oml_c, op0=ALU.mult, op1=ALU.add),
                     reads=["sigT", "oml", "noml"], writes=["kT"])
                P.op("dve", lambda h, hh=hh: h.tensor_tensor(out=BT[hh][s2][:], in0=kT[:], in1=enbT[:], op=ALU.mult),
                     reads=["kT", "enbT"], writes=[f"BT{hh}_{s2}"])
                for c in range(8):
                    cs = slice(c * 64, (c + 1) * 64)
                    P.op("act", lambda h, cs=cs, c=c: h.activation(out=edT[:, cs], in_=bT[:, cs], func=AF.Exp, scale=-1.0,
                                                                   bias=bT[:, c * 64 + 63:c * 64 + 64]), reads=["bT"], writes=["edT"])
                P.op("dve", lambda h: h.tensor_tensor(out=KdT[:], in0=kT[:], in1=edT[:], op=ALU.mult), reads=["kT", "edT"], writes=["KdT"])
                for tt in range(4):
                    tok = slice(tt * 128, (tt + 1) * 128)
                    P.op("pe", lambda h, tok=tok: h.transpose(out=ptr[:, tok], in_=KdT[:, tok], identity=ident[:]), reads=["KdT", "ident"], writes=["ptr"])
                P.op("dve", lambda h, hh=hh: h.tensor_copy(out=Kd[hh][s2][:], in_=ptr[:, 0:512].rearrange("p (t d) -> p t d", t=4)),
                     writes=["ptr", f"Kd{hh}_{s2}"])

        def recur(jb):
            s2 = jb % 2
            blk = slice(jb * 512, (jb + 1) * 512)
            for c_ in range(8):
                chunk(jb, s2, c_)
            for hh in range(2):
                row = (2 * rp + hh) * 128
                o = P.op("pool", lambda h, hh=hh, row=row: h.dma_start(out=yT_out[row:row + 128, blk], in_=yT_blk[hh][s2][:]),
                         reads=[f"yT{hh}_{s2}"], dma=True)
                out_ops.append(o)

        def chunk(jb, s2, c):
            if True:
                tt, half = c // 2, c % 2
                p0 = 64 * half
                pr = slice(p0, p0 + 64)
                cs = slice(c * 64, (c + 1) * 64)
                par = tt % 2
                cgl = jb * 8 + c
                cur, nxt = cgl % 2, (cgl + 1) % 2
                for hh in range(2):
                    hc = slice(hh * 128, (hh + 1) * 128)
                    sc = slice(hh * 64, (hh + 1) * 64)
                    P.op("pe", lambda h, hh=hh, sc=sc: h.matmul(psc[pr, sc], lhsT=BT[hh][s2][:, cs], rhs=AT[hh][s2][:, cs], start=True, stop=True),
                         reads=[f"BT{hh}_{s2}", f"AT{hh}_{s2}"], writes=["psc"])
                    P.op("dve", lambda h, hh=hh, sc=sc: h.tensor_tensor(out=scT[hh][pr, :], in0=psc[pr, sc], in1=tri[pr, :], op=ALU.mult),
                         reads=["tri"], writes=["psc", f"scT{hh}"])
                    P.op("pe", lambda h, hh=hh, hc=hc: h.matmul(po[par][pr, hc], lhsT=scT[hh][pr, :], rhs=inp_sb[s2][pr, tt, hc], start=True, stop=False),
                         reads=[f"scT{hh}", f"inp{s2}"], writes=[f"po{par}"])
                    P.op("pe", lambda h, hh=hh, hc=hc: h.matmul(po[par][pr, hc], lhsT=AT[hh][s2][:, cs], rhs=stb[hh][cur][:], start=False, stop=True),
                         reads=[f"AT{hh}_{s2}", f"stb{hh}_{cur}"], writes=[f"po{par}"])
                    P.op("pe", lambda h, hh=hh, hc=hc: h.matmul(pst[:, hc], lhsT=Kd[hh][s2][pr, tt, :], rhs=inp_sb[s2][pr, tt, hc], start=True, stop=True),
                         reads=[f"Kd{hh}_{s2}", f"inp{s2}"], writes=["pst"])
                    P.op("dve", lambda h, hh=hh, hc=hc: h.scalar_tensor_tensor(
                        out=stf[hh][:], in0=stf[hh][:], scalar=ebT[hh][s2][:, c * 64 + 63:c * 64 + 64], in1=pst[:, hc], op0=ALU.mult, op1=ALU.add),
                        reads=[f"ebT{hh}_{s2}"], writes=["pst", f"stf{hh}"])
                    P.op("act", lambda h, hh=hh: h.copy(out=stb[hh][nxt][:], in_=stf[hh][:]), reads=[f"stf{hh}"], writes=[f"stb{hh}_{nxt}"])
                if half == 1:
                    tok = slice(tt * 128, (tt + 1) * 128)
                    for hh in range(2):
                        hc = slice(hh * 128, (hh + 1) * 128)
                        yc = slice(hh * 128, (hh + 1) * 128)
                        P.op("act", lambda h, hc=hc: h.copy(out=osb[:], in_=po[par][:, hc]), writes=[f"po{par}", "osb"])
                        P.op("dve", lambda h: h.scalar_tensor_tensor(out=sq[:], in0=osb[:], scalar=1.0, in1=osb[:], op0=ALU.mult, op1=ALU.mult,
                                                                     accum_out=st[:, 0:1]), reads=["osb"], writes=["sq", "ss"])
                        P.op("dve", lambda h: h.tensor_scalar(out=st[:, 1:2], in0=st[:, 0:1], scalar1=1.0 / 128, scalar2=EPS, op0=ALU.mult, op1=ALU.add),
                             reads=["ss"], writes=["rt"])
                        P.op("pool", lambda h: h.tensor_tensor(out=st[:, 2:3], in0=st[:, 1:2], in1=mhalf[:], op=ALU.pow), reads=["rt", "mhalf"], writes=["rstd"])
                        P.op("dve", lambda h: h.scalar_tensor_tensor(out=t1[:], in0=osb[:], scalar=st[:, 2:3], in1=og_b[:], op0=ALU.mult, op1=ALU.mult),
                             reads=["osb", "rstd", "og_b"], writes=["t1"])
                        P.op("dve", lambda h, hc=hc: h.tensor_tensor(out=ybf[:], in0=t1[:], in1=sgt[s2][:, tt, hc], op=ALU.mult),
                             reads=["t1", f"sgt{s2}"], writes=["ybf"])
                        P.op("pe", lambda h, yc=yc: h.transpose(out=pty[:, yc], in_=ybf[:], identity=ident[:]), reads=["ybf", "ident"], writes=["pty"])
                        P.op("act", lambda h, hh=hh, yc=yc: h.copy(out=yT_blk[hh][s2][:, tok], in_=pty[:, yc]), writes=["pty", f"yT{hh}_{s2}"])

        P.replay(P.capture(lambda: prep(0)))
        for jb in range(NB):
            main = P.capture(lambda: recur(jb))
            side = P.capture(lambda: prep(jb + 1)) if jb + 1 < NB else []
            P.replay(main, side)
    return out_ops


def build_fused(S):
    nc = bass.Bass("TRN2", target_bir_lowering=False)
    NB = S // 512
    D = 1024
    dt = nc.dram_tensor
    x_d = dt("x", [S, D], F32, kind="ExternalInput").ap()
    pT0_d = dt("pT0", [256, S], F32, kind="ExternalInput").ap()
    pT1_d = dt("pT1", [256, S], F32, kind="ExternalInput").ap()
    gpre0_d = dt("gpre0", [128, 8], F32, kind="ExternalInput").ap()
    gpre1_d = dt("gpre1", [1, D], F32, kind="ExternalInput").ap()
    gpost0_d = dt("gpost0", [1, D], F32, kind="ExternalInput").ap()
    gpost1_d = dt("gpost1", [1, D], F32, kind="ExternalInput").ap()
    watt_d = dt("watt", [D, 4112], F32, kind="ExternalInput").ap()
    bf_d = dt("bf", [8, 2, 1], F32, kind="ExternalInput").ap()
    wo0_d = dt("wo0", [D, D], F32, kind="ExternalInput").ap()
    wrec_d = dt("wrec", [D, 4096], F32, kind="ExternalInput").ap()
    lb_d = dt("lbr", [128, 2, 8], F32, kind="ExternalInput").ap()
    og_d = dt("og", [1, 128], F32, kind="ExternalInput").ap()
    wo1_d = dt("wo1", [D, D], F32, kind="ExternalInput").ap()
    wg0_d = dt("wg0", [D, D], F32, kind="ExternalInput").ap()
    wg1_d = dt("wg1", [D, D], F32, kind="ExternalInput").ap()
    wp0_d = dt("wp0", [256, D], F32, kind="ExternalInput").ap()
    wp1_d = dt("wp1", [256, D], F32, kind="ExternalInput").ap()
    out_d = dt("out", [S, D], F32, kind="ExternalOutput").ap()
    aT_s = dt("aT_s", [D, S], BF16).ap()
    h1_s = dt("h1_s", [S, D], F32).ap()
    uT_s = dt("uT_s", [D, S], BF16).ap()
    yT_s = dt("yT_s", [D, S], BF16).ap()
    cscr = dt("cscr", [8 * NB, 2, 6, 512], BF16).ap()
    u0T_s = dt("u0T_s", [D, S], BF16).ap()
    with ExitStack() as ges:
        bar = ges.enter_context(nc.semaphore("phase_bar"))
        with ExitStack() as es:
            P = Prog(nc, es, bar=(bar, 0), sem_es=ges, tag="a")
            oo = emit_att(P, nc, es, S, 8, x_d, watt_d, gpre0_d, bf_d, aT_s, cscr, "a_", ucache=u0T_s)
            P.emit(final_wait_ops=oo)
        with ExitStack() as es:
            P = Prog(nc, es, bar=(bar, 1), sem_es=ges, tag="b")
            oo = emit_post(P, nc, es, S, True, aT_s, x_d, pT0_d, wo0_d, wg0_d, wp0_d, gpost0_d, gpre1_d, h1_s, uT_s, "b_")
            P.emit(final_wait_ops=oo)
        with ExitStack() as es:
            P = Prog(nc, es, bar=(bar, 2), sem_es=ges, tag="c")
            oo = emit_rec(P, nc, es, S, 4, uT_s, wrec_d, lb_d, og_d, yT_s, "c_")
            P.emit(final_wait_ops=oo)
        with ExitStack() as es:
            P = Prog(nc, es, bar=(bar, 3), sem_es=ges, tag="d")
            oo = emit_post(P, nc, es, S, False, yT_s, h1_s, pT1_d, wo1_d, wg1_d, wp1_d, gpost1_d, None, out_d, None, "d_")
            P.emit(final_wait_ops=oo)
    return nc


def kernel(x, p, norm_pre, norm_post, att_w_in, att_b_f, att_w_out, rec_w_in, rec_lb,
           rec_out_norm, rec_w_out, ple_w_proj, ple_w_gate):
    f32 = np.float32
    A = lambda a: np.ascontiguousarray(np.asarray(a, f32))
    x = np.asarray(x, f32); p = np.asarray(p, f32)
    B, S, D = x.shape
    shared = {
        "gpre0": A(np.asarray(norm_pre, f32)[0].reshape(8, 128).T), "gpre1": A(np.asarray(norm_pre, f32)[1][None]),
        "gpost0": A(np.asarray(norm_post, f32)[0][None]), "gpost1": A(np.asarray(norm_post, f32)[1][None]),
        "watt": A(np.asarray(att_w_in)[0]), "bf": A(np.asarray(att_b_f, f32)[0].reshape(8, 2, 1)),
        "wo0": A(np.asarray(att_w_out)[0]), "wrec": A(np.asarray(rec_w_in)[0]),
        "lbr": A(np.asarray(rec_lb, f32).reshape(2, 8, 128).transpose(2, 0, 1)),
        "og": A(np.asarray(rec_out_norm, f32)[0][None]), "wo1": A(np.asarray(rec_w_out)[0]),
        "wg0": A(np.asarray(ple_w_gate)[0]), "wg1": A(np.asarray(ple_w_gate)[1]),
        "wp0": A(np.asarray(ple_w_proj)[0]), "wp1": A(np.asarray(ple_w_proj)[1]),
    }
    maps = []
    for b in range(B):
        m = dict(shared)
        m["x"] = A(x[b]); m["pT0"] = A(p[0, b].T); m["pT1"] = A(p[1, b].T)
        maps.append(m)
    res = run_bass_kernel_spmd(build_fused(S), maps, core_ids=list(range(B))).results
    return np.stack([np.asarray(res[b]["out"], f32) for b in range(B)], axis=0)
```

```python
from concourse.bass_utils import run_bass_kernel_spmd

import numpy as np
import concourse.bass as bass
import concourse.mybir as mybir
from contextlib import ExitStack

F32 = mybir.dt.float32
BF16 = mybir.dt.bfloat16
AF = mybir.ActivationFunctionType
ALU = mybir.AluOpType
AX = mybir.AxisListType

ENGS = ("pe", "act", "dve", "pool", "sp")
N_DMA_SEMS = 7
EPOCH = 30000


class _Op:
    __slots__ = ("eng", "fn", "deps", "signal", "sig", "is_dma", "prewait", "idx")


class Prog:
    def __init__(self, nc, es, bar=None, sem_es=None, tag=""):
        self.nc = nc
        self.es = es
        self.sem_es = sem_es if sem_es is not None else es
        self.tag = tag
        self.bar = bar
        self.ops = []
        self.last_w = {}
        self.readers = {}
        self.cap = None

    def op(self, eng, fn, reads=(), writes=(), dma=False):
        o = _Op()
        o.eng, o.fn, o.is_dma, o.signal, o.sig, o.prewait = eng, fn, dma, dma, None, None
        o.deps = (tuple(reads), tuple(writes))
        o.idx = -1
        if self.cap is not None:
            self.cap.append(o)
        else:
            self._record(o)
        return o

    def capture(self, gen):
        prev, self.cap = self.cap, []
        try:
            gen()
            return self.cap
        finally:
            self.cap = prev

    def replay(self, main, side=()):
        main, side = list(main), list(side)
        nm, ns = len(main), len(side)
        put = self.cap.append if self.cap is not None else self._record
        j = 0
        for i, o in enumerate(main):
            put(o)
            tgt = ((i + 1) * ns) // max(nm, 1)
            while j < tgt:
                put(side[j])
                j += 1
        while j < ns:
            put(side[j])
            j += 1

    def _record(self, o):
        reads, writes = o.deps
        o.idx = len(self.ops)
        deps = set()
        for r in reads:
            w = self.last_w.get(r)
            if w is not None:
                deps.add(w)
        for w_ in writes:
            w = self.last_w.get(w_)
            if w is not None:
                deps.add(w)
            for rd in self.readers.get(w_, ()):
                deps.add(rd)
        deps.discard(o)
        o.deps = deps
        for d in deps:
            if not (d.eng == "pe" and o.eng == "pe"):
                d.signal = True
        for r in reads:
            self.readers.setdefault(r, []).append(o)
        for w_ in writes:
            self.last_w[w_] = o
            self.readers[w_] = []
        self.ops.append(o)

    def emit(self, final_wait_ops=()):
        nc, es, tag = self.nc, self.sem_es, self.tag
        cnt = {e: 0 for e in ENGS}
        n_ep = {e: 0 for e in ENGS}
        for o in self.ops:
            if o.signal and not o.is_dma:
                cnt[o.eng] += 1
        eng_sems = {}
        for e in ENGS:
            n = (cnt[e] + EPOCH - 1) // EPOCH
            eng_sems[e] = [es.enter_context(nc.semaphore(f"s{tag}_{e}_{i}")) for i in range(max(n, 1))]
        dma_engs = sorted({o.eng for o in self.ops if o.is_dma})
        dma_sems = {e: [es.enter_context(nc.semaphore(f"s{tag}_dma_{e}_{i}")) for i in range(N_DMA_SEMS)] for e in dma_engs}
        dma_uses = {e: [0] * N_DMA_SEMS for e in dma_engs}
        dma_k = {e: 0 for e in dma_engs}
        cnt = {e: 0 for e in ENGS}
        for o in self.ops:
            if o.is_dma:
                k = dma_k[o.eng]
                sems, uses = dma_sems[o.eng], dma_uses[o.eng]
                if uses[k] > 0:
                    o.prewait = (sems[k], 16 * uses[k])
                uses[k] += 1
                o.sig = (sems[k], 16 * uses[k], 16)
                dma_k[o.eng] = (k + 1) % N_DMA_SEMS
            elif o.signal:
                c = cnt[o.eng]
                cnt[o.eng] += 1
                o.sig = (eng_sems[o.eng][c // EPOCH], (c % EPOCH) + 1, 1)
        per = {e: [o for o in self.ops if o.eng == e] for e in ENGS}
        final = list(final_wait_ops)

        bar = self.bar

        def run(e, h):
            seen = {}
            if bar is not None and bar[1] > 0:
                h.wait_ge(bar[0], 5 * bar[1])
            def wait(sem, val):
                key = id(sem)
                if seen.get(key, 0) < val:
                    h.wait_ge(sem, val)
                    seen[key] = val
            for o in per[e]:
                for d in sorted(o.deps, key=lambda d: d.idx):
                    if d.eng == "pe" and e == "pe":
                        continue
                    wait(d.sig[0], d.sig[1])
                if o.prewait is not None:
                    wait(*o.prewait)
                inst = o.fn(h)
                if o.sig is not None:
                    inst.then_inc(o.sig[0], o.sig[2])
            if e == "sp":
                for o in final:
                    wait(o.sig[0], o.sig[1])
            if bar is not None:
                h.drain().then_inc(bar[0], 1)

        with nc.Block() as block:
            @block.tensor
            def _(h):
                run("pe", h)

            @block.scalar
            def _(h):
                run("act", h)

            @block.vector
            def _(h):
                run("dve", h)

            @block.gpsimd
            def _(h):
                run("pool", h)

            @block.sync
            def _(h):
                run("sp", h)


EPS = 1e-6


def make_ident(P, ident):
    P.op("pool", lambda h: h.memset(ident[:], 1.0), writes=["ident"])
    P.op("pool", lambda h: h.affine_select(out=ident[:], in_=ident[:], pattern=[[1, 128]],
                                           compare_op=ALU.is_equal, fill=0.0, base=0,
                                           channel_multiplier=-1),
         reads=["ident"], writes=["ident"])


def _mk(nc, es):
    def sb(name, shape, dt):
        return es.enter_context(nc.sbuf_tensor(name, shape, dt))

    def ps(name, shape, dt):
        return es.enter_context(nc.psum_tensor(name, shape, dt))
    return sb, ps


def emit_post(P, nc, es, T, emit_u, aT_d, x_d, pT_d, wo_d, wg_d, wp_d, gpost_d, gpre_d, h_out, uT_out, pfx):
    sb, ps = _mk(nc, es)
    NT = T // 128
    ident = sb(pfx + "ident", [128, 128], BF16)
    mhalf = sb(pfx + "mhalf", [128, 1], F32)
    wstage = sb(pfx + "wstage", [128, 8, 1024], F32)
    wo_bf = sb(pfx + "wo_bf", [128, 8, 1024], BF16)
    wg_bf = sb(pfx + "wg_bf", [128, 8, 1024], BF16)
    wp_bf = sb(pfx + "wp_bf", [128, 2, 1024], BF16)
    gpost_b = sb(pfx + "gpost_b", [128, 1024], F32)
    gpre_b = sb(pfx + "gpre_b", [128, 1024], F32)
    aT_blk = [sb(pfx + f"aT_blk{i}", [128, 8, 512], BF16) for i in range(2)]
    pT_st = [sb(pfx + f"pT_st{i}", [128, 2, 512], F32) for i in range(2)]
    pT_bf = [sb(pfx + f"pT_bf{i}", [128, 2, 512], BF16) for i in range(2)]
    uT_blk = [sb(pfx + f"uT_blk{i}", [128, 8, 512], BF16) for i in range(2)]
    x_sb = [sb(pfx + f"x_sb{i}", [128, 1024], F32) for i in range(2)]
    h1 = [sb(pfx + f"h1_{i}", [128, 1024], F32) for i in range(2)]
    hn = [sb(pfx + f"hn_{i}", [128, 1024], F32) for i in range(2)]
    h1bf = [sb(pfx + f"h1bf{i}", [128, 1024], BF16) for i in range(2)]
    ysb = sb(pfx + "ysb", [128, 1024], F32)
    sq = sb(pfx + "sq", [128, 1024], BF16)
    tmp = sb(pfx + "tmp", [128, 1024], F32)
    tmp2 = sb(pfx + "tmp2", [128, 1024], F32)
    h1T = sb(pfx + "h1T", [128, 1024], BF16)
    ubf = sb(pfx + "ubf", [128, 1024], BF16)
    gate = sb(pfx + "gate", [128, 1024], F32)
    st = sb(pfx + "st", [128, 8], F32)
    py = ps(pfx + "py", [128, 1024], F32)
    pg = ps(pfx + "pg", [128, 1024], F32)
    pp = ps(pfx + "pp", [128, 1024], F32)
    ptr = ps(pfx + "ptr", [128, 1024], BF16)
    ptr2 = ps(pfx + "ptr2", [128, 1024], BF16)

    make_ident(P, ident)
    P.op("pool", lambda h: h.memset(mhalf[:], -0.5), writes=["mhalf"])
    P.op("sp", lambda h: h.dma_start(out=gpost_b[:], in_=gpost_d.partition_broadcast(128)), writes=["gpost_b"], dma=True)
    if emit_u:
        P.op("sp", lambda h: h.dma_start(out=gpre_b[:], in_=gpre_d.partition_broadcast(128)), writes=["gpre_b"], dma=True)
    for (wd, wb, nm, nk) in ((wo_d, wo_bf, "wo_bf", 8), (wg_d, wg_bf, "wg_bf", 8), (wp_d, wp_bf, "wp_bf", 2)):
        P.op("sp", lambda h, wd=wd, nk=nk: h.dma_start(out=wstage[:, 0:nk, :], in_=wd.rearrange("(k p) n -> p k n", p=128)),
             writes=["wstage"], dma=True)
        P.op("dve", lambda h, wb=wb, nk=nk: h.tensor_copy(out=wb[:], in_=wstage[:, 0:nk, :]), reads=["wstage"], writes=[nm])
    aT_v = aT_d.rearrange("(k p) t -> p k t", p=128)
    pT_v = pT_d.rearrange("(k p) t -> p k t", p=128)
    uT_v = uT_out.rearrange("(k p) t -> p k t", p=128) if emit_u else None
    out_ops = []

    def rms(src, rd, c0, nm):
        P.op("dve", lambda h: h.scalar_tensor_tensor(out=sq[:], in0=src[:], scalar=1.0, in1=src[:], op0=ALU.mult, op1=ALU.mult,
                                                     accum_out=st[:, c0:c0 + 1]), reads=[rd], writes=["sq", nm + "ss"])
        P.op("dve", lambda h: h.tensor_scalar(out=st[:, c0 + 1:c0 + 2], in0=st[:, c0:c0 + 1], scalar1=1.0 / 1024, scalar2=EPS,
                                              op0=ALU.mult, op1=ALU.add), reads=[nm + "ss"], writes=[nm + "rt"])
        P.op("pool", lambda h: h.tensor_tensor(out=st[:, c0 + 2:c0 + 3], in0=st[:, c0 + 1:c0 + 2], in1=mhalf[:], op=ALU.pow),
             reads=[nm + "rt", "mhalf"], writes=[nm + "rstd"])

    def stage_a(t):
        s, b, tt = t % 2, t // 4, t % 4
        bs = b % 2
        tok = slice(tt * 128, (tt + 1) * 128)
        if tt == 0:
            P.op("sp", lambda h: h.dma_start(out=aT_blk[bs][:], in_=aT_v[:, :, b * 512:(b + 1) * 512]), writes=[f"aT{bs}"], dma=True)
            P.op("sp", lambda h: h.dma_start(out=pT_st[bs][:], in_=pT_v[:, :, b * 512:(b + 1) * 512]), writes=[f"pTst{bs}"], dma=True)
            P.op("pool", lambda h: h.tensor_copy(out=pT_bf[bs][:], in_=pT_st[bs][:]), reads=[f"pTst{bs}"], writes=[f"pTbf{bs}"])
        P.op("sp", lambda h: h.dma_start(out=x_sb[s][:], in_=x_d[t * 128:(t + 1) * 128, :]), writes=[f"x{s}"], dma=True)
        for half in range(2):
            hs = slice(half * 512, (half + 1) * 512)
            for k in range(8):
                P.op("pe", lambda h, k=k, hs=hs: h.matmul(py[:, hs], lhsT=aT_blk[bs][:, k, tok], rhs=wo_bf[:, k, hs], start=(k == 0), stop=(k == 7)),
                     reads=[f"aT{bs}", "wo_bf"], writes=["py"])
        P.op("act", lambda h: h.copy(out=ysb[:], in_=py[:]), reads=["py"], writes=["ysb"])
        rms(ysb, "ysb", 0, "a")
        P.op("dve", lambda h: h.scalar_tensor_tensor(out=tmp[:], in0=ysb[:], scalar=st[:, 2:3], in1=gpost_b[:], op0=ALU.mult, op1=ALU.mult),
             reads=["ysb", "arstd", "gpost_b"], writes=["tmp"])
        P.op("dve", lambda h: h.tensor_tensor(out=h1[s][:], in0=tmp[:], in1=x_sb[s][:], op=ALU.add), reads=["tmp", f"x{s}"], writes=[f"h1{s}"])
        P.op("act", lambda h: h.copy(out=h1bf[s][:], in_=h1[s][:]), reads=[f"h1{s}"], writes=[f"h1bf{s}"])

    def stage_b(t):
        s, b, tt = t % 2, t // 4, t % 4
        bs = b % 2
        tok = slice(tt * 128, (tt + 1) * 128)
        for k in range(8):
            ks = slice(k * 128, (k + 1) * 128)
            P.op("pe", lambda h, ks=ks: h.transpose(out=ptr[:, ks], in_=h1bf[s][:, ks], identity=ident[:]), reads=[f"h1bf{s}", "ident"], writes=["ptr"])
        P.op("dve", lambda h: h.tensor_copy(out=h1T[:], in_=ptr[:]), reads=["ptr"], writes=["h1T"])
        for half in range(2):
            hs = slice(half * 512, (half + 1) * 512)
            for k in range(8):
                ks = slice(k * 128, (k + 1) * 128)
                P.op("pe", lambda h, k=k, ks=ks, hs=hs: h.matmul(pg[:, hs], lhsT=h1T[:, ks], rhs=wg_bf[:, k, hs], start=(k == 0), stop=(k == 7)),
                     reads=["h1T", "wg_bf"], writes=["pg"])
        P.op("act", lambda h: h.activation(out=gate[:], in_=pg[:], func=AF.Sigmoid), reads=["pg"], writes=["gate"])
        for half in range(2):
            hs = slice(half * 512, (half + 1) * 512)
            for k in range(2):
                P.op("pe", lambda h, k=k, hs=hs: h.matmul(pp[:, hs], lhsT=pT_bf[bs][:, k, tok], rhs=wp_bf[:, k, hs], start=(k == 0), stop=(k == 1)),
                     reads=[f"pTbf{bs}", "wp_bf"], writes=["pp"])
        P.op("dve", lambda h: h.tensor_tensor(out=tmp2[:], in0=pp[:], in1=gate[:], op=ALU.mult), reads=["pp", "gate"], writes=["tmp2"])
        P.op("dve", lambda h: h.tensor_tensor(out=hn[s][:], in0=h1[s][:], in1=tmp2[:], op=ALU.add), reads=[f"h1{s}", "tmp2"], writes=[f"hn{s}"])
        o = P.op("pool", lambda h: h.dma_start(out=h_out[t * 128:(t + 1) * 128, :], in_=hn[s][:]), reads=[f"hn{s}"], dma=True)
        out_ops.append(o)
        if emit_u:
            rms(hn[s], f"hn{s}", 3, "c")
            P.op("dve", lambda h: h.scalar_tensor_tensor(out=ubf[:], in0=hn[s][:], scalar=st[:, 5:6], in1=gpre_b[:], op0=ALU.mult, op1=ALU.mult),
                 reads=[f"hn{s}", "crstd", "gpre_b"], writes=["ubf"])
            for k in range(8):
                ks = slice(k * 128, (k + 1) * 128)
                P.op("pe", lambda h, ks=ks: h.transpose(out=ptr2[:, ks], in_=ubf[:, ks], identity=ident[:]), reads=["ubf", "ident"], writes=["ptr2"])
            P.op("act", lambda h: h.copy(out=uT_blk[bs][:, :, tok], in_=ptr2[:].rearrange("p (k t) -> p k t", k=8)), reads=["ptr2"], writes=[f"uT{bs}"])
            if tt == 3:
                o = P.op("pool", lambda h: h.dma_start(out=uT_v[:, :, b * 512:(b + 1) * 512], in_=uT_blk[bs][:]), reads=[f"uT{bs}"], dma=True)
                out_ops.append(o)

    P.replay(P.capture(lambda: stage_a(0)))
    for t in range(NT):
        main = P.capture(lambda: stage_b(t))
        side = P.capture(lambda: stage_a(t + 1)) if t + 1 < NT else []
        P.replay(main, side)
    return out_ops


def emit_att(P, nc, es, S, n_pairs, x_d, w_d, gp_d, bf_d, aT_out, cscr, pfx, ucache=None):
    sb, ps = _mk(nc, es)
    NB = S // 512
    NKT = S // 128
    QB, KB, VB, GB, FB = 0, n_pairs * 128, 2 * n_pairs * 128, 3 * n_pairs * 128, 4 * n_pairs * 128
    ident = sb(pfx + "ident", [128, 128], BF16)
    mneg = sb(pfx + "mneg", [128, 128], F32)
    ones_f = sb(pfx + "ones_f", [128, 64], F32)
    ones2 = sb(pfx + "ones2", [2, 512], F32)
    gp = sb(pfx + "gp", [128, 8], F32)
    wst = sb(pfx + "wst", [128, 8, 514], F32)
    wsc = sb(pfx + "wsc", [128, 8, 514], BF16)
    KT = [sb(pfx + f"KT{h}", [70, S], BF16) for h in range(2)]
    Vt = sb(pfx + "Vt", [128, NKT, 2, 65], BF16)
    Qaug = [[sb(pfx + f"Qaug{h}_{s}", [70, 512], BF16) for s in range(2)] for h in range(2)]
    sgf = [sb(pfx + f"sgf{s}", [128, 512], BF16) for s in range(2)]
    sgB = [sb(pfx + f"sgB{s}", [64, 512], BF16) for s in range(2)]
    sg = [[sgf[s][0:64, :] for s in range(2)], [sgB[s][:] for s in range(2)]]
    qst = sb(pfx + "qst", [128, 512], BF16)
    kst = sb(pfx + "kst", [128, 512], BF16)
    x_sb = [sb(pfx + f"x_sb{i}", [128, 1024], F32) for i in range(2)]
    sq = sb(pfx + "sq", [128, 1024], BF16)
    usc = [sb(pfx + f"usc{i}", [128, 1024], BF16) for i in range(4)]
    uT = [sb(pfx + f"uT{i}", [128, 8, 512], BF16) for i in range(2)]
    st = sb(pfx + "st", [128, 16], F32)
    mhalf = sb(pfx + "mhalf", [128, 1], F32)
    eg = sb(pfx + "eg", [128, 512], F32)
    eg2 = sb(pfx + "eg2", [128, 512], F32)
    bt = sb(pfx + "bt", [2, 2], F32)
    e1 = sb(pfx + "e1", [2, 512], F32)
    spl = sb(pfx + "spl", [2, 512], F32)
    cpos = [sb(pfx + f"cpos{i}", [2, 512], F32) for i in range(2)]
    r1 = sb(pfx + "r1", [2, 512], F32)
    r2 = sb(pfx + "r2", [2, 512], F32)
    ST = sb(pfx + "ST", [2, 6, 512], BF16)
    Pt = [sb(pfx + f"Pt{i}", [128, 512], BF16) for i in range(3)]
    rden = sb(pfx + "rden", [128, 512], F32)
    tnorm = sb(pfx + "tnorm", [64, 512], F32)
    a_blk = [[sb(pfx + f"a_blk{h}_{s}", [64, 512], BF16) for s in range(2)] for h in range(2)]
    ptr = ps(pfx + "ptr", [128, 1024], BF16)
    pj = [ps(pfx + f"pj{i}", [128, 512], F32) for i in range(2)]
    ps_s = [ps(pfx + f"ps_s{i}", [128, 512], F32) for i in range(3)]
    po = [ps(pfx + f"po{i}", [128, 512], F32) for i in range(2)]

    make_ident(P, ident)
    P.op("pool", lambda h: h.memset(mneg[:], 3.0e38), writes=["mneg"])
    P.op("pool", lambda h: h.affine_select(out=mneg[:], in_=mneg[:], pattern=[[1, 128]], compare_op=ALU.is_ge, fill=-30000.0,
                                           base=0, channel_multiplier=-1), reads=["mneg"], writes=["mneg"])
    P.op("pool", lambda h: h.memset(ones_f[:], 1.0), writes=["ones_f"])
    P.op("pool", lambda h: h.memset(ones2[:], 1.0), writes=["ones2"])
    P.op("pool", lambda h: h.memset(mhalf[:], -0.5), writes=["mhalf"])
    P.op("sp", lambda h: h.dma_start(out=gp[:], in_=gp_d), writes=["gp"], dma=True)
    w_v = w_d.rearrange("(k p) n -> p k n", p=128)
    uc_v = ucache.rearrange("(k p) t -> p k t", p=128) if ucache is not None else None
    out_ops = []
    for pp in range(n_pairs):
        for gi_, base in enumerate((QB, KB, VB, GB)):
            P.op("sp", lambda h, gi_=gi_, base=base, pp=pp: h.dma_start(
                out=wst[:, :, gi_ * 128:(gi_ + 1) * 128], in_=w_v[:, :, base + pp * 128:base + (pp + 1) * 128]),
                writes=["wst"], dma=True)
        P.op("sp", lambda h, pp=pp: h.dma_start(out=wst[:, :, 512:514], in_=w_v[:, :, FB + 2 * pp:FB + 2 * pp + 2]),
             writes=["wst"], dma=True)
        for k in range(8):
            P.op("dve", lambda h, k=k: h.tensor_scalar(out=wsc[:, k, :], in0=wst[:, k, :], scalar1=gp[:, k:k + 1], scalar2=None, op0=ALU.mult),
                 reads=["wst", "gp"], writes=["wsc"])
        P.op("sp", lambda h, pp=pp: h.dma_start(out=bt[:, 0:1], in_=bf_d[pp]), writes=["bt"], dma=True)
        P.op("dve", lambda h: h.tensor_scalar(out=bt[:, 1:2], in0=bt[:, 0:1], scalar1=-1.0, scalar2=None, op0=ALU.mult),
             reads=["bt"], writes=["negb"])
        P.op("pool", lambda h: h.memset(Vt[:], 1.0), writes=[f"V_{kb}" for kb in range(NB)])
        for hh in range(2):
            P.op("pool", lambda h, hh=hh: h.memset(KT[hh][64:70, :], 1.0), writes=[f"KTaug{hh}_{kb}" for kb in range(NB)])
            for s in range(2):
                P.op("pool", lambda h, hh=hh, s=s: h.memset(Qaug[hh][s][64:70, :], 1.0), writes=[f"Qaug_aug{hh}_{s}"])
        def prep(jb):
            s2 = jb % 2
            blk = slice(jb * 512, (jb + 1) * 512)
            if ucache is not None and pp > 0:
                P.op("sp", lambda h, s2=s2, blk=blk: h.dma_start(out=uT[s2][:], in_=uc_v[:, :, blk]), reads=[f"ucache{jb}"], writes=[f"uT{s2}"], dma=True)
            else:
                for tt in range(4):
                    t = jb * 4 + tt
                    s = t % 2
                    c = 4 * tt
                    P.op("sp", lambda h, t=t, s=s: h.dma_start(out=x_sb[s][:], in_=x_d[t * 128:(t + 1) * 128, :]), writes=[f"x{s}"], dma=True)
                    P.op("dve", lambda h, s=s, c=c: h.scalar_tensor_tensor(out=sq[:], in0=x_sb[s][:], scalar=1.0, in1=x_sb[s][:],
                                                                           op0=ALU.mult, op1=ALU.mult, accum_out=st[:, c:c + 1]),
                         reads=[f"x{s}"], writes=["sq", f"ss{tt}"])
                    P.op("dve", lambda h, c=c: h.tensor_scalar(out=st[:, c + 1:c + 2], in0=st[:, c:c + 1], scalar1=1.0 / 1024, scalar2=EPS,
                                                               op0=ALU.mult, op1=ALU.add), reads=[f"ss{tt}"], writes=[f"rt{tt}"])
                    P.op("pool", lambda h, c=c: h.tensor_tensor(out=st[:, c + 2:c + 3], in0=st[:, c + 1:c + 2], in1=mhalf[:], op=ALU.pow),
                         reads=[f"rt{tt}", "mhalf"], writes=[f"rstd{tt}"])
                    P.op("dve", lambda h, s=s, c=c, tt=tt: h.tensor_scalar(out=usc[tt][:], in0=x_sb[s][:], scalar1=st[:, c + 2:c + 3], scalar2=None, op0=ALU.mult),
                         reads=[f"x{s}", f"rstd{tt}"], writes=[f"usc{tt}"])
                for tt in range(4):
                    tok = slice(tt * 128, (tt + 1) * 128)
                    for k in range(8):
                        ks = slice(k * 128, (k + 1) * 128)
                        P.op("pe", lambda h, ks=ks, tt=tt: h.transpose(out=ptr[:, ks], in_=usc[tt][:, ks], identity=ident[:]),
                             reads=[f"usc{tt}", "ident"], writes=["ptr"])
                    P.op("dve", lambda h, s2=s2, tok=tok: h.tensor_copy(out=uT[s2][:, :, tok], in_=ptr[:].rearrange("p (k t) -> p k t", k=8)),
                         reads=["ptr"], writes=[f"uT{s2}"])
                if ucache is not None:
                    P.op("sp", lambda h, s2=s2, blk=blk: h.dma_start(out=uc_v[:, :, blk], in_=uT[s2][:]), reads=[f"uT{s2}"], writes=[f"ucache{jb}"], dma=True)
            gi = 0
            for kind in ("q", "k", "g"):
                base = {"q": 0, "k": 128, "g": 384}[kind]
                pjb = pj[gi % 2]
                pjn = f"pj{gi % 2}"
                gi += 1
                for k in range(8):
                    P.op("pe", lambda h, k=k, base=base, pjb=pjb, s2=s2: h.matmul(
                        pjb[:, :], lhsT=wsc[:, k, base:base + 128], rhs=uT[s2][:, k, :], start=(k == 0), stop=(k == 7)),
                        reads=["wsc", f"uT{s2}"], writes=[pjn])
                if kind == "q":
                    P.op("dve", lambda h, pjb=pjb, s2=s2: h.tensor_scalar(out=Qaug[0][s2][0:64, :], in0=pjb[0:64, :], scalar1=0.125, scalar2=None, op0=ALU.mult),
                         reads=[pjn], writes=[f"Qaug_m0_{s2}"])
                    P.op("dve", lambda h, pjb=pjb: h.tensor_scalar(out=qst[64:128, :], in0=pjb[64:128, :], scalar1=0.125, scalar2=None, op0=ALU.mult),
                         reads=[pjn], writes=["qst"])
                    P.op("pool", lambda h, s2=s2: h.dma_start(out=Qaug[1][s2][0:64, :], in_=qst[64:128, :]), reads=["qst"], writes=[f"Qaug_m1_{s2}"], dma=True)
                elif kind == "k":
                    P.op("dve", lambda h, pjb=pjb, blk=blk: h.tensor_copy(out=KT[0][0:64, blk], in_=pjb[0:64, :]), reads=[pjn], writes=[f"KT0_{jb}"])
                    P.op("dve", lambda h, pjb=pjb: h.tensor_copy(out=kst[64:128, :], in_=pjb[64:128, :]), reads=[pjn], writes=["kst"])
                    P.op("pool", lambda h, blk=blk: h.dma_start(out=KT[1][0:64, blk], in_=kst[64:128, :]), reads=["kst"], writes=[f"KT1_{jb}"], dma=True)
                else:
                    P.op("act", lambda h, pjb=pjb: h.activation(out=eg[:], in_=pjb[:, :], func=AF.Exp, scale=-1.0), reads=[pjn], writes=["eg"])
                    P.op("act", lambda h: h.activation(out=eg2[:], in_=eg[:], func=AF.Ln, scale=1.0, bias=1.0), reads=["eg"], writes=["eg2"])
                    P.op("act", lambda h: h.activation(out=eg[:], in_=eg2[:], func=AF.Exp, scale=-1.0), reads=["eg2"], writes=["eg"])
                    P.op("dve", lambda h, pjb=pjb, s2=s2: h.tensor_tensor(out=sgf[s2][:], in0=pjb[:, :], in1=eg[:], op=ALU.mult),
                         reads=[pjn, "eg"], writes=[f"sg0_{s2}", f"sgf_{s2}"])
                    P.op("pool", lambda h, s2=s2: h.dma_start(out=sg[1][s2][:], in_=sgf[s2][64:128, :]), reads=[f"sgf_{s2}"], writes=[f"sg1_{s2}"], dma=True)
            pjb = pj[gi % 2]
            pjn = f"pj{gi % 2}"
            gi += 1
            for tt in range(4):
                tok = slice(tt * 128, (tt + 1) * 128)
                for k in range(8):
                    P.op("pe", lambda h, k=k, pjb=pjb, s2=s2, tok=tok: h.matmul(
                        pjb[:, tok], lhsT=uT[s2][:, k, tok], rhs=wsc[:, k, 256:384], start=(k == 0), stop=(k == 7)),
                        reads=["wsc", f"uT{s2}"], writes=[pjn])
            P.op("dve", lambda h, pjb=pjb, jb=jb: h.tensor_copy(out=Vt[:, jb * 4:(jb + 1) * 4, :, 0:64],
                                                                 in_=pjb[:].rearrange("p (t h d) -> p t h d", t=4, h=2)),
                 reads=[pjn], writes=[f"V_{jb}"])
            pjb = pj[gi % 2]
            pjn = f"pj{gi % 2}"
            gi += 1
            for k in range(8):
                P.op("pe", lambda h, k=k, pjb=pjb, s2=s2: h.matmul(
                    pjb[0:2, :], lhsT=wsc[:, k, 512:514], rhs=uT[s2][:, k, :], start=(k == 0), stop=(k == 7)),
                    reads=["wsc", f"uT{s2}"], writes=[pjn])
            P.op("act", lambda h, pjb=pjb: h.activation(out=e1[:], in_=pjb[0:2, :], func=AF.Exp, scale=-1.0, bias=bt[:, 1:2]),
                 reads=[pjn, "negb"], writes=["e1"])
            P.op("act", lambda h: h.activation(out=spl[:], in_=e1[:], func=AF.Ln, scale=1.0, bias=1.0), reads=["e1"], writes=["spl"])
            init = 0.0 if jb == 0 else cpos[1 - s2][:, 511:512]
            P.op("dve", lambda h, s2=s2, init=init: h.tensor_tensor_scan(out=cpos[s2][:], data0=ones2[:], data1=spl[:], initial=init,
                                                                         op0=ALU.mult, op1=ALU.add),
                 reads=["ones2", "spl", f"cpos{1 - s2}"], writes=[f"cpos{s2}"])
            P.op("dve", lambda h, s2=s2: h.tensor_copy(out=ST[:, 3, :], in_=cpos[s2][:]), reads=[f"cpos{s2}"], writes=["ST3"])
            P.op("dve", lambda h, s2=s2: h.tensor_tensor(out=r1[:], in0=cpos[s2][:], in1=ST[:, 3, :], op=ALU.subtract), reads=[f"cpos{s2}", "ST3"], writes=["r1"])
            P.op("dve", lambda h: h.tensor_copy(out=ST[:, 4, :], in_=r1[:]), reads=["r1"], writes=["ST4"])
            P.op("dve", lambda h: h.tensor_tensor(out=r2[:], in0=r1[:], in1=ST[:, 4, :], op=ALU.subtract), reads=["r1", "ST4"], writes=["r2"])
            P.op("dve", lambda h: h.tensor_copy(out=ST[:, 5, :], in_=r2[:]), reads=["r2"], writes=["ST5"])
            P.op("dve", lambda h: h.tensor_scalar(out=ST[:, 0:3, :], in0=ST[:, 3:6, :], scalar1=-1.0, scalar2=None, op0=ALU.mult),
                 reads=["ST3", "ST4", "ST5"], writes=["ST0"])
            ci = pp * NB + jb
            P.op("pool", lambda h, ci=ci: h.dma_start(out=cscr[ci], in_=ST[:]), reads=["ST0", "ST3", "ST4", "ST5"], writes=[f"cscr{ci}"], dma=True)
            for hh in range(2):
                P.op("pool", lambda h, ci=ci, hh=hh, s2=s2: h.dma_start(out=Qaug[hh][s2][64:67, :], in_=cscr[ci, hh, 0:3, :]),
                     reads=[f"cscr{ci}"], writes=[f"Qaug_aug{hh}_{s2}"], dma=True)
                P.op("pool", lambda h, ci=ci, hh=hh, blk=blk: h.dma_start(out=KT[hh][67:70, blk], in_=cscr[ci, hh, 3:6, :]),
                     reads=[f"cscr{ci}"], writes=[f"KTaug{hh}_{jb}"], dma=True)

        def attend(jb):
            s2 = jb % 2
            nkt = 4 * (jb + 1)
            items = [(hh, kt) for kt in range(nkt) for hh in range(2)]
            n = len(items)
            for i in range(n + 2):
                if i < n:
                    hh, kt = items[i]
                    r = kt - 4 * jb
                    c0 = max(r, 0) * 128
                    kb = kt // 4
                    sl = i % 3
                    P.op("pe", lambda h, hh=hh, kt=kt, c0=c0, sl=sl, s2=s2: h.matmul(
                        ps_s[sl][:, c0:512], lhsT=KT[hh][0:70, kt * 128:(kt + 1) * 128], rhs=Qaug[hh][s2][0:70, c0:512], start=True, stop=True),
                        reads=[f"KT{hh}_{kb}", f"KTaug{hh}_{kb}", f"Qaug_m{hh}_{s2}", f"Qaug_aug{hh}_{s2}"], writes=[f"ps_s{sl}"])
                    if r >= 0:
                        P.op("dve", lambda h, sl=sl, c0=c0: h.tensor_tensor(out=ps_s[sl][:, c0:c0 + 128], in0=ps_s[sl][:, c0:c0 + 128], in1=mneg[:], op=ALU.min),
                             reads=[f"ps_s{sl}", "mneg"], writes=[f"ps_s{sl}"])
                j = i - 2
                if j >= 0:
                    hh, kt = items[j]
                    r = kt - 4 * jb
                    c0 = max(r, 0) * 128
                    kb = kt // 4
                    sl = j % 3
                    P.op("act", lambda h, sl=sl, c0=c0: h.activation(out=Pt[sl][:, c0:512], in_=ps_s[sl][:, c0:512], func=AF.Exp),
                         reads=[f"ps_s{sl}"], writes=[f"Pt{sl}"])
                    P.op("pe", lambda h, hh=hh, kt=kt, c0=c0, sl=sl, nkt=nkt: h.matmul(
                        po[hh][0:65, c0:512], lhsT=Vt[:, kt, hh, 0:65], rhs=Pt[sl][:, c0:512], start=(kt == 0), stop=(kt == nkt - 1)),
                        reads=[f"V_{kb}", f"Pt{sl}"], writes=[f"po{hh}"])

        def finish(jb):
            s2 = jb % 2
            blk = slice(jb * 512, (jb + 1) * 512)
            for hh in range(2):
                P.op("act", lambda h, hh=hh: h.activation(out=rden[64:65, :], in_=po[hh][64:65, :], func=AF.Ln), reads=[f"po{hh}"], writes=["rden"])
                P.op("act", lambda h: h.activation(out=rden[64:65, :], in_=rden[64:65, :], func=AF.Exp, scale=-1.0), reads=["rden"], writes=["rden"])
                P.op("pe", lambda h: h.matmul(pj[0][0:64, :], lhsT=ones_f[64:65, 0:64], rhs=rden[64:65, :], start=True, stop=True),
                     reads=["ones_f", "rden"], writes=["pj0"])
                P.op("dve", lambda h, hh=hh, s2=s2: h.tensor_tensor(out=tnorm[:], in0=pj[0][0:64, :], in1=sg[hh][s2], op=ALU.mult),
                     reads=["pj0", f"sg{hh}_{s2}"], writes=["tnorm"])
                P.op("dve", lambda h, hh=hh, s2=s2: h.tensor_tensor(out=a_blk[hh][s2][:], in0=po[hh][0:64, :], in1=tnorm[:], op=ALU.mult),
                     reads=[f"po{hh}", "tnorm"], writes=[f"a_blk{hh}_{s2}"])
                row = (2 * pp + hh) * 64
                o = P.op("pool", lambda h, hh=hh, s2=s2, row=row, blk=blk: h.dma_start(out=aT_out[row:row + 64, blk], in_=a_blk[hh][s2][:]),
                         reads=[f"a_blk{hh}_{s2}"], dma=True)
                out_ops.append(o)

        P.replay(P.capture(lambda: prep(0)))
        for jb in range(NB):
            main = P.capture(lambda: attend(jb))
            side = P.capture(lambda: prep(jb + 1)) if jb + 1 < NB else []
            P.replay(main, side)
            finish(jb)
    return out_ops


def emit_rec(P, nc, es, S, n_pairs, uT_d, w_d, lb_d, og_d, yT_out, pfx):
    sb, ps = _mk(nc, es)
    NB = S // 512
    NH = 2 * n_pairs
    W = NH * 128
    ident = sb(pfx + "ident", [128, 128], BF16)
    tri = sb(pfx + "tri", [128, 512], F32)
    rmask = sb(pfx + "rmask", [128, 512], F32)
    mhalf = sb(pfx + "mhalf", [128, 1], F32)
    wst = sb(pfx + "wst", [128, 8, 256], F32)
    wbf = sb(pfx + "wbf", [128, 8, 1024], BF16)
    lbt = sb(pfx + "lbt", [128, 2, NH], F32)
    lbv = sb(pfx + "lbv", [128, 4, NH], F32)
    og_b = sb(pfx + "og_b", [128, 128], F32)
    uTb = [sb(pfx + f"uTb{i}", [128, 8, 512], BF16) for i in range(2)]
    sigT = sb(pfx + "sigT", [128, 512], F32)
    sgt1 = sb(pfx + "esgT", [128, 512], F32)
    fT = sb(pfx + "fT", [128, 512], F32)
    lfT = sb(pfx + "lfT", [128, 512], F32)
    bT = sb(pfx + "bT", [128, 512], F32)
    enbT = sb(pfx + "enbT", [128, 512], F32)
    edT = sb(pfx + "edT", [128, 512], F32)
    kT = sb(pfx + "kT", [128, 512], F32)
    KdT = sb(pfx + "KdT", [128, 512], BF16)
    ebT = [[sb(pfx + f"ebT{h}_{s}", [128, 512], F32) for s in range(2)] for h in range(2)]
    AT = [[sb(pfx + f"AT{h}_{s}", [128, 512], BF16) for s in range(2)] for h in range(2)]
    BT = [[sb(pfx + f"BT{h}_{s}", [128, 512], BF16) for s in range(2)] for h in range(2)]
    Kd = [[sb(pfx + f"Kd{h}_{s}", [128, 4, 128], BF16) for s in range(2)] for h in range(2)]
    inp_sb = [sb(pfx + f"inp_sb{s}", [128, 4, 256], BF16) for s in range(2)]
    sgt = [sb(pfx + f"sgt{s}", [128, 4, 256], BF16) for s in range(2)]
    eg = sb(pfx + "eg", [128, 256], F32)
    eg2 = sb(pfx + "eg2", [128, 256], F32)
    scT = sb(pfx + "scT", [128, 512], BF16)
    stf = [sb(pfx + f"stf{h}", [128, 128], F32) for h in range(2)]
    stb = [[sb(pfx + f"stb{h}_{s}", [128, 128], BF16) for s in range(2)] for h in range(2)]
    osb = sb(pfx + "osb", [128, 128], F32)
    sq = sb(pfx + "sq", [128, 128], BF16)
    st = sb(pfx + "st", [128, 4], F32)
    t1 = sb(pfx + "t1", [128, 128], F32)
    ybf = sb(pfx + "ybf", [128, 128], BF16)
    yT_blk = [[sb(pfx + f"yT_blk{h}_{s}", [128, 512], BF16) for s in range(2)] for h in range(2)]
    pjf = ps(pfx + "pjf", [128, 512], F32)
    pjt = ps(pfx + "pjt", [128, 512], F32)
    psc = ps(pfx + "psc", [128, 512], F32)
    po = [ps(pfx + f"po{i}", [128, 512], F32) for i in range(2)]
    pst = ps(pfx + "pst", [128, 512], F32)
    ptr = ps(pfx + "ptr", [128, 1024], BF16)
    pty = ps(pfx + "pty", [128, 1024], BF16)

    make_ident(P, ident)
    P.op("pool", lambda h: h.memset(tri[:], 1.0), writes=["tri"])
    for half in range(2):
        P.op("pool", lambda h, half=half: h.affine_select(
            out=tri[half * 64:(half + 1) * 64, :], in_=tri[half * 64:(half + 1) * 64, :], pattern=[[0, 8], [1, 64]],
            compare_op=ALU.is_ge, fill=0.0, base=0, channel_multiplier=-1), reads=["tri"], writes=["tri"])
    P.op("pool", lambda h: h.memset(rmask[:], 1.0), writes=["rmask"])
    P.op("pool", lambda h: h.memset(rmask[:].rearrange("p (c t) -> p c t", t=64)[:, :, 0:1], 0.0), reads=["rmask"], writes=["rmask"])
    P.op("pool", lambda h: h.memset(mhalf[:], -0.5), writes=["mhalf"])
    P.op("sp", lambda h: h.dma_start(out=og_b[:], in_=og_d.partition_broadcast(128)), writes=["og_b"], dma=True)
    P.op("sp", lambda h: h.dma_start(out=lbt[:], in_=lb_d), writes=["lbt"], dma=True)
    P.op("dve", lambda h: h.tensor_tensor(out=lbv[:, 0, :], in0=lbt[:, 0, :], in1=lbt[:, 1, :], op=ALU.subtract), reads=["lbt"], writes=["lbdiff"])
    P.op("act", lambda h: h.activation(out=lbv[:, 0, :], in_=lbv[:, 0, :], func=AF.Exp), reads=["lbdiff"], writes=["lbdiff"])
    P.op("act", lambda h: h.activation(out=lbv[:, 0, :], in_=lbv[:, 0, :], func=AF.Ln, scale=1.0, bias=1.0), reads=["lbdiff"], writes=["lbdiff"])
    P.op("act", lambda h: h.activation(out=lbv[:, 1, :], in_=lbv[:, 0, :], func=AF.Exp, scale=-1.0), reads=["lbdiff"], writes=["lb"])
    P.op("dve", lambda h: h.tensor_scalar(out=lbv[:, 2, :], in0=lbv[:, 1, :], scalar1=-1.0, scalar2=1.0, op0=ALU.mult, op1=ALU.add),
         reads=["lb"], writes=["oml"])
    P.op("dve", lambda h: h.tensor_scalar(out=lbv[:, 3, :], in0=lbv[:, 1, :], scalar1=-1.0, scalar2=None, op0=ALU.add), reads=["lb"], writes=["noml"])
    w_v = w_d.rearrange("(k p) n -> p k n", p=128)
    uT_v = uT_d.rearrange("(k p) t -> p k t", p=128)
    out_ops = []
    for rp in range(n_pairs):
        for g in range(4):
            P.op("sp", lambda h, g=g, rp=rp: h.dma_start(out=wst[:], in_=w_v[:, :, g * W + rp * 256:g * W + (rp + 1) * 256]), writes=["wst"], dma=True)
            P.op("dve", lambda h, g=g: h.tensor_copy(out=wbf[:, :, g * 256:(g + 1) * 256], in_=wst[:]), reads=["wst"], writes=["wbf"])
        for hh in range(2):
            P.op("pool", lambda h, hh=hh: h.memset(stf[hh][:], 0.0), writes=[f"stf{hh}"])
            P.op("pool", lambda h, hh=hh: h.memset(stb[hh][0][:], 0.0), writes=[f"stb{hh}_0"])

        def prep(jb):
            s2 = jb % 2
            blk = slice(jb * 512, (jb + 1) * 512)
            P.op("sp", lambda h: h.dma_start(out=uTb[s2][:], in_=uT_v[:, :, blk]), writes=[f"uTb{s2}"], dma=True)
            for tt in range(4):
                tok = slice(tt * 128, (tt + 1) * 128)
                for k in range(8):
                    P.op("pe", lambda h, k=k, tok=tok: h.matmul(pjt[:], lhsT=uTb[s2][:, k, tok], rhs=wbf[:, k, 512:1024], start=(k == 0), stop=(k == 7)),
                         reads=["wbf", f"uTb{s2}"], writes=["pjt"])
                P.op("dve", lambda h, tt=tt: h.tensor_copy(out=inp_sb[s2][:, tt, :], in_=pjt[:, 0:256]), writes=["pjt", f"inp{s2}"])
                P.op("act", lambda h: h.activation(out=eg[:], in_=pjt[:, 256:512], func=AF.Exp, scale=-1.0), writes=["pjt", "eg"])
                P.op("act", lambda h: h.activation(out=eg2[:], in_=eg[:], func=AF.Ln, scale=1.0, bias=1.0), reads=["eg"], writes=["eg2"])
                P.op("act", lambda h: h.activation(out=eg[:], in_=eg2[:], func=AF.Exp, scale=-1.0), reads=["eg2"], writes=["eg"])
                P.op("dve", lambda h, tt=tt: h.tensor_tensor(out=sgt[s2][:, tt, :], in0=pjt[:, 256:512], in1=eg[:], op=ALU.mult),
                     reads=["eg"], writes=["pjt", f"sgt{s2}"])
            for hh in range(2):
                hd = 2 * rp + hh
                lb_c, oml_c, noml_c = lbv[:, 1, hd:hd + 1], lbv[:, 2, hd:hd + 1], lbv[:, 3, hd:hd + 1]
                fcol = 256 + hh * 128
                qcol = hh * 128
                for k in range(8):
                    P.op("pe", lambda h, k=k, fcol=fcol: h.matmul(pjf[:], lhsT=wbf[:, k, fcol:fcol + 128], rhs=uTb[s2][:, k, :], start=(k == 0), stop=(k == 7)),
                         reads=["wbf", f"uTb{s2}"], writes=["pjf"])
                P.op("act", lambda h: h.activation(out=sgt1[:], in_=pjf[:], func=AF.Exp, scale=-1.0), writes=["pjf", "sgt1"])
                for k in range(8):
                    P.op("pe", lambda h, k=k, qcol=qcol: h.matmul(pjf[:], lhsT=wbf[:, k, qcol:qcol + 128], rhs=uTb[s2][:, k, :], start=(k == 0), stop=(k == 7)),
                         reads=["wbf", f"uTb{s2}"], writes=["pjf"])
                P.op("act", lambda h: h.activation(out=sigT[:], in_=sgt1[:], func=AF.Ln, scale=1.0, bias=1.0), reads=["sgt1"], writes=["sigT"])
                P.op("act", lambda h: h.activation(out=sigT[:], in_=sigT[:], func=AF.Exp, scale=-1.0), reads=["sigT"], writes=["sigT"])
                P.op("dve", lambda h, oml_c=oml_c, lb_c=lb_c: h.tensor_scalar(out=fT[:], in0=sigT[:], scalar1=oml_c, scalar2=lb_c, op0=ALU.mult, op1=ALU.add),
                     reads=["sigT", "oml", "lb"], writes=["fT"])
                P.op("act", lambda h: h.activation(out=lfT[:], in_=fT[:], func=AF.Ln), reads=["fT"], writes=["lfT"])
                P.op("dve", lambda h: h.tensor_tensor_scan(out=bT[:], data0=rmask[:], data1=lfT[:], initial=0.0, op0=ALU.mult, op1=ALU.add),
                     reads=["rmask", "lfT"], writes=["bT"])
                P.op("act", lambda h, hh=hh: h.activation(out=ebT[hh][s2][:], in_=bT[:], func=AF.Exp), reads=["bT"], writes=[f"ebT{hh}_{s2}"])
                P.op("act", lambda h: h.activation(out=enbT[:], in_=bT[:], func=AF.Exp, scale=-1.0), reads=["bT"], writes=["enbT"])
                P.op("dve", lambda h, hh=hh: h.tensor_tensor(out=AT[hh][s2][:], in0=pjf[:], in1=ebT[hh][s2][:], op=ALU.mult),
                     reads=[f"ebT{hh}_{s2}"], writes=["pjf", f"AT{hh}_{s2}"])
                P.op("dve", lambda h, oml_c=oml_c, noml_c=noml_c: h.tensor_scalar(out=kT[:], in0=sigT[:], scalar1=noml_c, scalar2=oml_c, op0=ALU.mult, op1=ALU.add),
                     reads=["sigT", "oml", "noml"], writes=["kT"])
                P.op("dve", lambda h, hh=hh: h.tensor_tensor(out=BT[hh][s2][:], in0=kT[:], in1=enbT[:], op=ALU.mult),
                     reads=["kT", "enbT"], writes=[f"BT{hh}_{s2}"])
                for c in range(8):
                    cs = slice(c * 64, (c + 1) * 64)
                    P.op("act", lambda h, cs=cs, c=c: h.activation(out=edT[:, cs], in_=bT[:, cs], func=AF.Exp, scale=-1.0,
                                                                   bias=bT[:, c * 64 + 63:c * 64 + 64]), reads=["bT"], writes=["edT"])
                P.op("dve", lambda h: h.tensor_tensor(out=KdT[:], in0=kT[:], in1=edT[:], op=ALU.mult), reads=["kT", "edT"], writes=["KdT"])
                for tt in range(4):
                    tok = slice(tt * 128, (tt + 1) * 128)
                    P.op("pe", lambda h, tok=tok: h.transpose(out=ptr[:, tok], in_=KdT[:, tok], identity=ident[:]), reads=["KdT", "ident"], writes=["ptr"])
                P.op("dve", lambda h, hh=hh: h.tensor_copy(out=Kd[hh][s2][:], in_=ptr[:, 0:512].rearrange("p (t d) -> p t d", t=4)),
                     writes=["ptr", f"Kd{hh}_{s2}"])

        def recur(jb):
            s2 = jb % 2
            blk = slice(jb * 512, (jb + 1) * 512)
            for c_ in range(8):
                for hh in range(2):
                    pr_ = slice(64 * (c_ % 2), 64 * (c_ % 2) + 64)
                    sl_ = slice(((c_ // 2) * 2 + hh) * 64, ((c_ // 2) * 2 + hh + 1) * 64)
                    cs_ = slice(c_ * 64, (c_ + 1) * 64)
                    P.op("pe", lambda h, hh=hh, pr_=pr_, sl_=sl_, cs_=cs_: h.matmul(psc[pr_, sl_], lhsT=BT[hh][s2][:, cs_], rhs=AT[hh][s2][:, cs_],
                                                                                start=True, stop=True),
                         reads=[f"BT{hh}_{s2}", f"AT{hh}_{s2}"], writes=["psc"])
            P.op("dve", lambda h: h.tensor_tensor(out=scT[:], in0=psc[:], in1=tri[:], op=ALU.mult), reads=["tri"], writes=["psc", "scT"])
            for tt_ in range(4):
                main_ = P.capture(lambda: (chunk(jb, s2, 2 * tt_), chunk(jb, s2, 2 * tt_ + 1)))
                side_ = P.capture(lambda: norm(jb, s2, tt_ - 1)) if tt_ >= 1 else []
                P.replay(main_, side_)
            norm(jb, s2, 3)
            for hh in range(2):
                row = (2 * rp + hh) * 128
                o = P.op("pool", lambda h, hh=hh, row=row: h.dma_start(out=yT_out[row:row + 128, blk], in_=yT_blk[hh][s2][:]),
                         reads=[f"yT{hh}_{s2}"], dma=True)
                out_ops.append(o)

        def chunk(jb, s2, c):
            if True:
                tt, half = c // 2, c % 2
                p0 = 64 * half
                pr = slice(p0, p0 + 64)
                cs = slice(c * 64, (c + 1) * 64)
                par = tt % 2
                cgl = jb * 8 + c
                cur, nxt = cgl % 2, (cgl + 1) % 2
                for hh in range(2):
                    hc = slice(hh * 128, (hh + 1) * 128)
                    sc = slice(hh * 64, (hh + 1) * 64)
                    sl = slice((tt * 2 + hh) * 64, (tt * 2 + hh + 1) * 64)
                    P.op("pe", lambda h, hh=hh, hc=hc, sl=sl: h.matmul(po[par][pr, hc], lhsT=scT[pr, sl], rhs=inp_sb[s2][pr, tt, hc], start=True, stop=False),
                         reads=["scT", f"inp{s2}"], writes=[f"po{par}"])
                    P.op("pe", lambda h, hh=hh, hc=hc: h.matmul(po[par][pr, hc], lhsT=AT[hh][s2][:, cs], rhs=stb[hh][cur][:], start=False, stop=True),
                         reads=[f"AT{hh}_{s2}", f"stb{hh}_{cur}"], writes=[f"po{par}"])
                    P.op("pe", lambda h, hh=hh, hc=hc: h.matmul(pst[:, hc], lhsT=Kd[hh][s2][pr, tt, :], rhs=inp_sb[s2][pr, tt, hc], start=True, stop=True),
                         reads=[f"Kd{hh}_{s2}", f"inp{s2}"], writes=["pst"])
                    P.op("dve", lambda h, hh=hh, hc=hc: h.scalar_tensor_tensor(
                        out=stf[hh][:], in0=stf[hh][:], scalar=ebT[hh][s2][:, c * 64 + 63:c * 64 + 64], in1=pst[:, hc], op0=ALU.mult, op1=ALU.add),
                        reads=[f"ebT{hh}_{s2}"], writes=["pst", f"stf{hh}"])
                    P.op("act", lambda h, hh=hh: h.copy(out=stb[hh][nxt][:], in_=stf[hh][:]), reads=[f"stf{hh}"], writes=[f"stb{hh}_{nxt}"])

        def norm(jb, s2, tt):
            if True:
                if True:
                    par = tt % 2
                    tok = slice(tt * 128, (tt + 1) * 128)
                    for hh in range(2):
                        hc = slice(hh * 128, (hh + 1) * 128)
                        yc = slice(hh * 128, (hh + 1) * 128)
                        P.op("act", lambda h, hc=hc: h.copy(out=osb[:], in_=po[par][:, hc]), writes=[f"po{par}", "osb"])
                        P.op("dve", lambda h: h.scalar_tensor_tensor(out=sq[:], in0=osb[:], scalar=1.0, in1=osb[:], op0=ALU.mult, op1=ALU.mult,
                                                                     accum_out=st[:, 0:1]), reads=["osb"], writes=["sq", "ss"])
                        P.op("dve", lambda h: h.tensor_scalar(out=st[:, 1:2], in0=st[:, 0:1], scalar1=1.0 / 128, scalar2=EPS, op0=ALU.mult, op1=ALU.add),
                             reads=["ss"], writes=["rt"])
                        P.op("pool", lambda h: h.tensor_tensor(out=st[:, 2:3], in0=st[:, 1:2], in1=mhalf[:], op=ALU.pow), reads=["rt", "mhalf"], writes=["rstd"])
                        P.op("dve", lambda h: h.scalar_tensor_tensor(out=t1[:], in0=osb[:], scalar=st[:, 2:3], in1=og_b[:], op0=ALU.mult, op1=ALU.mult),
                             reads=["osb", "rstd", "og_b"], writes=["t1"])
                        P.op("dve", lambda h, hc=hc: h.tensor_tensor(out=ybf[:], in0=t1[:], in1=sgt[s2][:, tt, hc], op=ALU.mult),
                             reads=["t1", f"sgt{s2}"], writes=["ybf"])
                        P.op("pe", lambda h, yc=yc: h.transpose(out=pty[:, yc], in_=ybf[:], identity=ident[:]), reads=["ybf", "ident"], writes=["pty"])
                        P.op("act", lambda h, hh=hh, yc=yc: h.copy(out=yT_blk[hh][s2][:, tok], in_=pty[:, yc]), writes=["pty", f"yT{hh}_{s2}"])

        P.replay(P.capture(lambda: prep(0)))
        for jb in range(NB):
            main = P.capture(lambda: recur(jb))
            side = P.capture(lambda: prep(jb + 1)) if jb + 1 < NB else []
            P.replay(main, side)
    return out_ops


def build_fused(S):
    nc = bass.Bass("TRN2", target_bir_lowering=False)
    NB = S // 512
    D = 1024
    dt = nc.dram_tensor
    x_d = dt("x", [S, D], F32, kind="ExternalInput").ap()
    pT0_d = dt("pT0", [256, S], F32, kind="ExternalInput").ap()
    pT1_d = dt("pT1", [256, S], F32, kind="ExternalInput").ap()
    gpre0_d = dt("gpre0", [128, 8], F32, kind="ExternalInput").ap()
    gpre1_d = dt("gpre1", [1, D], F32, kind="ExternalInput").ap()
    gpost0_d = dt("gpost0", [1, D], F32, kind="ExternalInput").ap()
    gpost1_d = dt("gpost1", [1, D], F32, kind="ExternalInput").ap()
    watt_d = dt("watt", [D, 4112], F32, kind="ExternalInput").ap()
    bf_d = dt("bf", [8, 2, 1], F32, kind="ExternalInput").ap()
    wo0_d = dt("wo0", [D, D], F32, kind="ExternalInput").ap()
    wrec_d = dt("wrec", [D, 4096], F32, kind="ExternalInput").ap()
    lb_d = dt("lbr", [128, 2, 8], F32, kind="ExternalInput").ap()
    og_d = dt("og", [1, 128], F32, kind="ExternalInput").ap()
    wo1_d = dt("wo1", [D, D], F32, kind="ExternalInput").ap()
    wg0_d = dt("wg0", [D, D], F32, kind="ExternalInput").ap()
    wg1_d = dt("wg1", [D, D], F32, kind="ExternalInput").ap()
    wp0_d = dt("wp0", [256, D], F32, kind="ExternalInput").ap()
    wp1_d = dt("wp1", [256, D], F32, kind="ExternalInput").ap()
    out_d = dt("out", [S, D], F32, kind="ExternalOutput").ap()
    aT_s = dt("aT_s", [D, S], BF16).ap()
    h1_s = dt("h1_s", [S, D], F32).ap()
    uT_s = dt("uT_s", [D, S], BF16).ap()
    yT_s = dt("yT_s", [D, S], BF16).ap()
    cscr = dt("cscr", [8 * NB, 2, 6, 512], BF16).ap()
    u0T_s = dt("u0T_s", [D, S], BF16).ap()
    with ExitStack() as ges:
        bar = ges.enter_context(nc.semaphore("phase_bar"))
        with ExitStack() as es:
            P = Prog(nc, es, bar=(bar, 0), sem_es=ges, tag="a")
            oo = emit_att(P, nc, es, S, 8, x_d, watt_d, gpre0_d, bf_d, aT_s, cscr, "a_", ucache=u0T_s)
            P.emit(final_wait_ops=oo)
        with ExitStack() as es:
            P = Prog(nc, es, bar=(bar, 1), sem_es=ges, tag="b")
            oo = emit_post(P, nc, es, S, True, aT_s, x_d, pT0_d, wo0_d, wg0_d, wp0_d, gpost0_d, gpre1_d, h1_s, uT_s, "b_")
            P.emit(final_wait_ops=oo)
        with ExitStack() as es:
            P = Prog(nc, es, bar=(bar, 2), sem_es=ges, tag="c")
            oo = emit_rec(P, nc, es, S, 4, uT_s, wrec_d, lb_d, og_d, yT_s, "c_")
            P.emit(final_wait_ops=oo)
        with ExitStack() as es:
            P = Prog(nc, es, bar=(bar, 3), sem_es=ges, tag="d")
            oo = emit_post(P, nc, es, S, False, yT_s, h1_s, pT1_d, wo1_d, wg1_d, wp1_d, gpost1_d, None, out_d, None, "d_")
            P.emit(final_wait_ops=oo)
    return nc


def kernel(x, p, norm_pre, norm_post, att_w_in, att_b_f, att_w_out, rec_w_in, rec_lb,
           rec_out_norm, rec_w_out, ple_w_proj, ple_w_gate):
    f32 = np.float32
    A = lambda a: np.ascontiguousarray(np.asarray(a, f32))
    x = np.asarray(x, f32); p = np.asarray(p, f32)
    B, S, D = x.shape
    shared = {
        "gpre0": A(np.asarray(norm_pre, f32)[0].reshape(8, 128).T), "gpre1": A(np.asarray(norm_pre, f32)[1][None]),
        "gpost0": A(np.asarray(norm_post, f32)[0][None]), "gpost1": A(np.asarray(norm_post, f32)[1][None]),
        "watt": A(np.asarray(att_w_in)[0]), "bf": A(np.asarray(att_b_f, f32)[0].reshape(8, 2, 1)),
        "wo0": A(np.asarray(att_w_out)[0]), "wrec": A(np.asarray(rec_w_in)[0]),
        "lbr": A(np.asarray(rec_lb, f32).reshape(2, 8, 128).transpose(2, 0, 1)),
        "og": A(np.asarray(rec_out_norm, f32)[0][None]), "wo1": A(np.asarray(rec_w_out)[0]),
        "wg0": A(np.asarray(ple_w_gate)[0]), "wg1": A(np.asarray(ple_w_gate)[1]),
        "wp0": A(np.asarray(ple_w_proj)[0]), "wp1": A(np.asarray(ple_w_proj)[1]),
    }
    maps = []
    for b in range(B):
        m = dict(shared)
        m["x"] = A(x[b]); m["pT0"] = A(p[0, b].T); m["pT1"] = A(p[1, b].T)
        maps.append(m)
    res = run_bass_kernel_spmd(build_fused(S), maps, core_ids=list(range(B))).results
    return np.stack([np.asarray(res[b]["out"], f32) for b in range(B)], axis=0)
```

```python
from concourse.bass_utils import run_bass_kernel_spmd

import numpy as np
import concourse.bass as bass
import concourse.mybir as mybir
from contextlib import ExitStack

F32 = mybir.dt.float32
BF16 = mybir.dt.bfloat16
AF = mybir.ActivationFunctionType
ALU = mybir.AluOpType
AX = mybir.AxisListType

ENGS = ("pe", "act", "dve", "pool", "sp")
N_DMA_SEMS = 7
EPOCH = 30000


class _Op:
    __slots__ = ("eng", "fn", "deps", "signal", "sig", "is_dma", "prewait", "idx")


class Prog:
    def __init__(self, nc, es, bar=None, sem_es=None, tag=""):
        self.nc = nc
        self.es = es
        self.sem_es = sem_es if sem_es is not None else es
        self.tag = tag
        self.bar = bar
        self.ops = []
        self.last_w = {}
        self.readers = {}
        self.cap = None

    def op(self, eng, fn, reads=(), writes=(), dma=False):
        o = _Op()
        o.eng, o.fn, o.is_dma, o.signal, o.sig, o.prewait = eng, fn, dma, dma, None, None
        o.deps = (tuple(reads), tuple(writes))
        o.idx = -1
        if self.cap is not None:
            self.cap.append(o)
        else:
            self._record(o)
        return o

    def capture(self, gen):
        prev, self.cap = self.cap, []
        try:
            gen()
            return self.cap
        finally:
            self.cap = prev

    def replay(self, main, side=()):
        main, side = list(main), list(side)
        nm, ns = len(main), len(side)
        put = self.cap.append if self.cap is not None else self._record
        j = 0
        for i, o in enumerate(main):
            put(o)
            tgt = ((i + 1) * ns) // max(nm, 1)
            while j < tgt:
                put(side[j])
                j += 1
        while j < ns:
            put(side[j])
            j += 1

    def _record(self, o):
        reads, writes = o.deps
        o.idx = len(self.ops)
        deps = set()
        for r in reads:
            w = self.last_w.get(r)
            if w is not None:
                deps.add(w)
        for w_ in writes:
            w = self.last_w.get(w_)
            if w is not None:
                deps.add(w)
            for rd in self.readers.get(w_, ()):
                deps.add(rd)
        deps.discard(o)
        o.deps = deps
        for d in deps:
            if not (d.eng == "pe" and o.eng == "pe"):
                d.signal = True
        for r in reads:
            self.readers.setdefault(r, []).append(o)
        for w_ in writes:
            self.last_w[w_] = o
            self.readers[w_] = []
        self.ops.append(o)

    def emit(self, final_wait_ops=()):
        nc, es, tag = self.nc, self.sem_es, self.tag
        cnt = {e: 0 for e in ENGS}
        n_ep = {e: 0 for e in ENGS}
        for o in self.ops:
            if o.signal and not o.is_dma:
                cnt[o.eng] += 1
        eng_sems = {}
        for e in ENGS:
            n = (cnt[e] + EPOCH - 1) // EPOCH
            eng_sems[e] = [es.enter_context(nc.semaphore(f"s{tag}_{e}_{i}")) for i in range(max(n, 1))]
        dma_engs = sorted({o.eng for o in self.ops if o.is_dma})
        dma_sems = {e: [es.enter_context(nc.semaphore(f"s{tag}_dma_{e}_{i}")) for i in range(N_DMA_SEMS)] for e in dma_engs}
        dma_uses = {e: [0] * N_DMA_SEMS for e in dma_engs}
        dma_k = {e: 0 for e in dma_engs}
        cnt = {e: 0 for e in ENGS}
        for o in self.ops:
            if o.is_dma:
                k = dma_k[o.eng]
                sems, uses = dma_sems[o.eng], dma_uses[o.eng]
                if uses[k] > 0:
                    o.prewait = (sems[k], 16 * uses[k])
                uses[k] += 1
                o.sig = (sems[k], 16 * uses[k], 16)
                dma_k[o.eng] = (k + 1) % N_DMA_SEMS
            elif o.signal:
                c = cnt[o.eng]
                cnt[o.eng] += 1
                o.sig = (eng_sems[o.eng][c // EPOCH], (c % EPOCH) + 1, 1)
        per = {e: [o for o in self.ops if o.eng == e] for e in ENGS}
        final = list(final_wait_ops)

        bar = self.bar

        def run(e, h):
            seen = {}
            if bar is not None and bar[1] > 0:
                h.wait_ge(bar[0], 5 * bar[1])
            def wait(sem, val):
                key = id(sem)
                if seen.get(key, 0) < val:
                    h.wait_ge(sem, val)
                    seen[key] = val
            for o in per[e]:
                for d in sorted(o.deps, key=lambda d: d.idx):
                    if d.eng == "pe" and e == "pe":
                        continue
                    wait(d.sig[0], d.sig[1])
                if o.prewait is not None:
                    wait(*o.prewait)
                inst = o.fn(h)
                if o.sig is not None:
                    inst.then_inc(o.sig[0], o.sig[2])
            if e == "sp":
                for o in final:
                    wait(o.sig[0], o.sig[1])
            if bar is not None:
                h.drain().then_inc(bar[0], 1)

        with nc.Block() as block:
            @block.tensor
            def _(h):
                run("pe", h)

            @block.scalar
            def _(h):
                run("act", h)

            @block.vector
            def _(h):
                run("dve", h)

            @block.gpsimd
            def _(h):
                run("pool", h)

            @block.sync
            def _(h):
                run("sp", h)


EPS = 1e-6


def make_ident(P, ident):
    P.op("pool", lambda h: h.memset(ident[:], 1.0), writes=["ident"])
    P.op("pool", lambda h: h.affine_select(out=ident[:], in_=ident[:], pattern=[[1, 128]],
                                           compare_op=ALU.is_equal, fill=0.0, base=0,
                                           channel_multiplier=-1),
         reads=["ident"], writes=["ident"])


def _mk(nc, es):
    def sb(name, shape, dt):
        return es.enter_context(nc.sbuf_tensor(name, shape, dt))

    def ps(name, shape, dt):
        return es.enter_context(nc.psum_tensor(name, shape, dt))
    return sb, ps


def emit_post(P, nc, es, T, emit_u, aT_d, x_d, pT_d, wo_d, wg_d, wp_d, gpost_d, gpre_d, h_out, uT_out, pfx):
    sb, ps = _mk(nc, es)
    NT = T // 128
    ident = sb(pfx + "ident", [128, 128], BF16)
    mhalf = sb(pfx + "mhalf", [128, 1], F32)
    wstage = sb(pfx + "wstage", [128, 8, 1024], F32)
    wo_bf = sb(pfx + "wo_bf", [128, 8, 1024], BF16)
    wg_bf = sb(pfx + "wg_bf", [128, 8, 1024], BF16)
    wp_bf = sb(pfx + "wp_bf", [128, 2, 1024], BF16)
    gpost_b = sb(pfx + "gpost_b", [128, 1024], F32)
    gpre_b = sb(pfx + "gpre_b", [128, 1024], F32)
    aT_blk = [sb(pfx + f"aT_blk{i}", [128, 8, 512], BF16) for i in range(2)]
    pT_st = [sb(pfx + f"pT_st{i}", [128, 2, 512], F32) for i in range(2)]
    pT_bf = [sb(pfx + f"pT_bf{i}", [128, 2, 512], BF16) for i in range(2)]
    uT_blk = [sb(pfx + f"uT_blk{i}", [128, 8, 512], BF16) for i in range(2)]
    x_sb = [sb(pfx + f"x_sb{i}", [128, 1024], F32) for i in range(2)]
    h1 = [sb(pfx + f"h1_{i}", [128, 1024], F32) for i in range(2)]
    hn = [sb(pfx + f"hn_{i}", [128, 1024], F32) for i in range(2)]
    h1bf = [sb(pfx + f"h1bf{i}", [128, 1024], BF16) for i in range(2)]
    ysb = sb(pfx + "ysb", [128, 1024], F32)
    sq = sb(pfx + "sq", [128, 1024], BF16)
    tmp = sb(pfx + "tmp", [128, 1024], F32)
    tmp2 = sb(pfx + "tmp2", [128, 1024], F32)
    h1T = sb(pfx + "h1T", [128, 1024], BF16)
    ubf = sb(pfx + "ubf", [128, 1024], BF16)
    gate = sb(pfx + "gate", [128, 1024], F32)
    st = sb(pfx + "st", [128, 8], F32)
    py = ps(pfx + "py", [128, 1024], F32)
    pg = ps(pfx + "pg", [128, 1024], F32)
    pp = ps(pfx + "pp", [128, 1024], F32)
    ptr = ps(pfx + "ptr", [128, 1024], BF16)
    ptr2 = ps(pfx + "ptr2", [128, 1024], BF16)

    make_ident(P, ident)
    P.op("pool", lambda h: h.memset(mhalf[:], -0.5), writes=["mhalf"])
    P.op("sp", lambda h: h.dma_start(out=gpost_b[:], in_=gpost_d.partition_broadcast(128)), writes=["gpost_b"], dma=True)
    if emit_u:
        P.op("sp", lambda h: h.dma_start(out=gpre_b[:], in_=gpre_d.partition_broadcast(128)), writes=["gpre_b"], dma=True)
    for (wd, wb, nm, nk) in ((wo_d, wo_bf, "wo_bf", 8), (wg_d, wg_bf, "wg_bf", 8), (wp_d, wp_bf, "wp_bf", 2)):
        P.op("sp", lambda h, wd=wd, nk=nk: h.dma_start(out=wstage[:, 0:nk, :], in_=wd.rearrange("(k p) n -> p k n", p=128)),
             writes=["wstage"], dma=True)
        P.op("dve", lambda h, wb=wb, nk=nk: h.tensor_copy(out=wb[:], in_=wstage[:, 0:nk, :]), reads=["wstage"], writes=[nm])
    aT_v = aT_d.rearrange("(k p) t -> p k t", p=128)
    pT_v = pT_d.rearrange("(k p) t -> p k t", p=128)
    uT_v = uT_out.rearrange("(k p) t -> p k t", p=128) if emit_u else None
    out_ops = []

    def rms(src, rd, c0, nm):
        P.op("dve", lambda h: h.scalar_tensor_tensor(out=sq[:], in0=src[:], scalar=1.0, in1=src[:], op0=ALU.mult, op1=ALU.mult,
                                                     accum_out=st[:, c0:c0 + 1]), reads=[rd], writes=["sq", nm + "ss"])
        P.op("dve", lambda h: h.tensor_scalar(out=st[:, c0 + 1:c0 + 2], in0=st[:, c0:c0 + 1], scalar1=1.0 / 1024, scalar2=EPS,
                                              op0=ALU.mult, op1=ALU.add), reads=[nm + "ss"], writes=[nm + "rt"])
        P.op("pool", lambda h: h.tensor_tensor(out=st[:, c0 + 2:c0 + 3], in0=st[:, c0 + 1:c0 + 2], in1=mhalf[:], op=ALU.pow),
             reads=[nm + "rt", "mhalf"], writes=[nm + "rstd"])

    def stage_a(t):
        s, b, tt = t % 2, t // 4, t % 4
        bs = b % 2
        tok = slice(tt * 128, (tt + 1) * 128)
        if tt == 0:
            P.op("sp", lambda h: h.dma_start(out=aT_blk[bs][:], in_=aT_v[:, :, b * 512:(b + 1) * 512]), writes=[f"aT{bs}"], dma=True)
            P.op("sp", lambda h: h.dma_start(out=pT_st[bs][:], in_=pT_v[:, :, b * 512:(b + 1) * 512]), writes=[f"pTst{bs}"], dma=True)
            P.op("pool", lambda h: h.tensor_copy(out=pT_bf[bs][:], in_=pT_st[bs][:]), reads=[f"pTst{bs}"], writes=[f"pTbf{bs}"])
        P.op("sp", lambda h: h.dma_start(out=x_sb[s][:], in_=x_d[t * 128:(t + 1) * 128, :]), writes=[f"x{s}"], dma=True)
        for half in range(2):
            hs = slice(half * 512, (half + 1) * 512)
            for k in range(8):
                P.op("pe", lambda h, k=k, hs=hs: h.matmul(py[:, hs], lhsT=aT_blk[bs][:, k, tok], rhs=wo_bf[:, k, hs], start=(k == 0), stop=(k == 7)),
                     reads=[f"aT{bs}", "wo_bf"], writes=["py"])
        P.op("act", lambda h: h.copy(out=ysb[:], in_=py[:]), reads=["py"], writes=["ysb"])
        rms(ysb, "ysb", 0, "a")
        P.op("dve", lambda h: h.scalar_tensor_tensor(out=tmp[:], in0=ysb[:], scalar=st[:, 2:3], in1=gpost_b[:], op0=ALU.mult, op1=ALU.mult),
             reads=["ysb", "arstd", "gpost_b"], writes=["tmp"])
        P.op("dve", lambda h: h.tensor_tensor(out=h1[s][:], in0=tmp[:], in1=x_sb[s][:], op=ALU.add), reads=["tmp", f"x{s}"], writes=[f"h1{s}"])
        P.op("act", lambda h: h.copy(out=h1bf[s][:], in_=h1[s][:]), reads=[f"h1{s}"], writes=[f"h1bf{s}"])

    def stage_b(t):
        s, b, tt = t % 2, t // 4, t % 4
        bs = b % 2
        tok = slice(tt * 128, (tt + 1) * 128)
        for k in range(8):
            ks = slice(k * 128, (k + 1) * 128)
            P.op("pe", lambda h, ks=ks: h.transpose(out=ptr[:, ks], in_=h1bf[s][:, ks], identity=ident[:]), reads=[f"h1bf{s}", "ident"], writes=["ptr"])
        P.op("dve", lambda h: h.tensor_copy(out=h1T[:], in_=ptr[:]), reads=["ptr"], writes=["h1T"])
        for half in range(2):
            hs = slice(half * 512, (half + 1) * 512)
            for k in range(8):
                ks = slice(k * 128, (k + 1) * 128)
                P.op("pe", lambda h, k=k, ks=ks, hs=hs: h.matmul(pg[:, hs], lhsT=h1T[:, ks], rhs=wg_bf[:, k, hs], start=(k == 0), stop=(k == 7)),
                     reads=["h1T", "wg_bf"], writes=["pg"])
        P.op("act", lambda h: h.activation(out=gate[:], in_=pg[:], func=AF.Sigmoid), reads=["pg"], writes=["gate"])
        for half in range(2):
            hs = slice(half * 512, (half + 1) * 512)
            for k in range(2):
                P.op("pe", lambda h, k=k, hs=hs: h.matmul(pp[:, hs], lhsT=pT_bf[bs][:, k, tok], rhs=wp_bf[:, k, hs], start=(k == 0), stop=(k == 1)),
                     reads=[f"pTbf{bs}", "wp_bf"], writes=["pp"])
        P.op("dve", lambda h: h.tensor_tensor(out=tmp2[:], in0=pp[:], in1=gate[:], op=ALU.mult), reads=["pp", "gate"], writes=["tmp2"])
        P.op("dve", lambda h: h.tensor_tensor(out=hn[s][:], in0=h1[s][:], in1=tmp2[:], op=ALU.add), reads=[f"h1{s}", "tmp2"], writes=[f"hn{s}"])
        o = P.op("pool", lambda h: h.dma_start(out=h_out[t * 128:(t + 1) * 128, :], in_=hn[s][:]), reads=[f"hn{s}"], dma=True)
        out_ops.append(o)
        if emit_u:
            rms(hn[s], f"hn{s}", 3, "c")
            P.op("dve", lambda h: h.scalar_tensor_tensor(out=ubf[:], in0=hn[s][:], scalar=st[:, 5:6], in1=gpre_b[:], op0=ALU.mult, op1=ALU.mult),
                 reads=[f"hn{s}", "crstd", "gpre_b"], writes=["ubf"])
            for k in range(8):
                ks = slice(k * 128, (k + 1) * 128)
                P.op("pe", lambda h, ks=ks: h.transpose(out=ptr2[:, ks], in_=ubf[:, ks], identity=ident[:]), reads=["ubf", "ident"], writes=["ptr2"])
            P.op("act", lambda h: h.copy(out=uT_blk[bs][:, :, tok], in_=ptr2[:].rearrange("p (k t) -> p k t", k=8)), reads=["ptr2"], writes=[f"uT{bs}"])
            if tt == 3:
                o = P.op("pool", lambda h: h.dma_start(out=uT_v[:, :, b * 512:(b + 1) * 512], in_=uT_blk[bs][:]), reads=[f"uT{bs}"], dma=True)
                out_ops.append(o)

    P.replay(P.capture(lambda: stage_a(0)))
    for t in range(NT):
        main = P.capture(lambda: stage_b(t))
        side = P.capture(lambda: stage_a(t + 1)) if t + 1 < NT else []
        P.replay(main, side)
    return out_ops


def emit_att(P, nc, es, S, n_pairs, x_d, w_d, gp_d, bf_d, aT_out, cscr, pfx, ucache=None):
    sb, ps = _mk(nc, es)
    NB = S // 512
    NKT = S // 128
    QB, KB, VB, GB, FB = 0, n_pairs * 128, 2 * n_pairs * 128, 3 * n_pairs * 128, 4 * n_pairs * 128
    ident = sb(pfx + "ident", [128, 128], BF16)
    mneg = sb(pfx + "mneg", [128, 128], F32)
    ones_f = sb(pfx + "ones_f", [128, 64], F32)
    ones2 = sb(pfx + "ones2", [2, 512], F32)
    gp = sb(pfx + "gp", [128, 8], F32)
    wst = sb(pfx + "wst", [128, 8, 514], F32)
    wsc = sb(pfx + "wsc", [128, 8, 514], BF16)
    KT = [sb(pfx + f"KT{h}", [70, S], BF16) for h in range(2)]
    Vt = sb(pfx + "Vt", [128, NKT, 2, 65], BF16)
    Qaug = [[sb(pfx + f"Qaug{h}_{s}", [70, 512], BF16) for s in range(2)] for h in range(2)]
    sgf = [sb(pfx + f"sgf{s}", [128, 512], BF16) for s in range(2)]
    sgB = [sb(pfx + f"sgB{s}", [64, 512], BF16) for s in range(2)]
    sg = [[sgf[s][0:64, :] for s in range(2)], [sgB[s][:] for s in range(2)]]
    qst = sb(pfx + "qst", [128, 512], BF16)
    kst = sb(pfx + "kst", [128, 512], BF16)
    x_sb = [sb(pfx + f"x_sb{i}", [128, 1024], F32) for i in range(2)]
    sq = sb(pfx + "sq", [128, 1024], BF16)
    usc = [sb(pfx + f"usc{i}", [128, 1024], BF16) for i in range(4)]
    uT = [sb(pfx + f"uT{i}", [128, 8, 512], BF16) for i in range(2)]
    st = sb(pfx + "st", [128, 16], F32)
    mhalf = sb(pfx + "mhalf", [128, 1], F32)
    eg = sb(pfx + "eg", [128, 512], F32)
    eg2 = sb(pfx + "eg2", [128, 512], F32)
    bt = sb(pfx + "bt", [2, 2], F32)
    e1 = sb(pfx + "e1", [2, 512], F32)
    spl = sb(pfx + "spl", [2, 512], F32)
    cpos = [sb(pfx + f"cpos{i}", [2, 512], F32) for i in range(2)]
    r1 = sb(pfx + "r1", [2, 512], F32)
    r2 = sb(pfx + "r2", [2, 512], F32)
    ST = sb(pfx + "ST", [2, 6, 512], BF16)
    Pt = [sb(pfx + f"Pt{i}", [128, 512], BF16) for i in range(3)]
    rden = sb(pfx + "rden", [128, 512], F32)
    tnorm = sb(pfx + "tnorm", [64, 512], F32)
    a_blk = [[sb(pfx + f"a_blk{h}_{s}", [64, 512], BF16) for s in range(2)] for h in range(2)]
    ptr = ps(pfx + "ptr", [128, 1024], BF16)
    pj = [ps(pfx + f"pj{i}", [128, 512], F32) for i in range(2)]
    ps_s = [ps(pfx + f"ps_s{i}", [128, 512], F32) for i in range(3)]
    po = [ps(pfx + f"po{i}", [128, 512], F32) for i in range(2)]

    make_ident(P, ident)
    P.op("pool", lambda h: h.memset(mneg[:], 3.0e38), writes=["mneg"])
    P.op("pool", lambda h: h.affine_select(out=mneg[:], in_=mneg[:], pattern=[[1, 128]], compare_op=ALU.is_ge, fill=-30000.0,
                                           base=0, channel_multiplier=-1), reads=["mneg"], writes=["mneg"])
    P.op("pool", lambda h: h.memset(ones_f[:], 1.0), writes=["ones_f"])
    P.op("pool", lambda h: h.memset(ones2[:], 1.0), writes=["ones2"])
    P.op("pool", lambda h: h.memset(mhalf[:], -0.5), writes=["mhalf"])
    P.op("sp", lambda h: h.dma_start(out=gp[:], in_=gp_d), writes=["gp"], dma=True)
    w_v = w_d.rearrange("(k p) n -> p k n", p=128)
    uc_v = ucache.rearrange("(k p) t -> p k t", p=128) if ucache is not None else None
    out_ops = []
    for pp in range(n_pairs):
        for gi_, base in enumerate((QB, KB, VB, GB)):
            P.op("sp", lambda h, gi_=gi_, base=base, pp=pp: h.dma_start(
                out=wst[:, :, gi_ * 128:(gi_ + 1) * 128], in_=w_v[:, :, base + pp * 128:base + (pp + 1) * 128]),
                writes=["wst"], dma=True)
        P.op("sp", lambda h, pp=pp: h.dma_start(out=wst[:, :, 512:514], in_=w_v[:, :, FB + 2 * pp:FB + 2 * pp + 2]),
             writes=["wst"], dma=True)
        for k in range(8):
            P.op("dve", lambda h, k=k: h.tensor_scalar(out=wsc[:, k, :], in0=wst[:, k, :], scalar1=gp[:, k:k + 1], scalar2=None, op0=ALU.mult),
                 reads=["wst", "gp"], writes=["wsc"])
        P.op("sp", lambda h, pp=pp: h.dma_start(out=bt[:, 0:1], in_=bf_d[pp]), writes=["bt"], dma=True)
        P.op("dve", lambda h: h.tensor_scalar(out=bt[:, 1:2], in0=bt[:, 0:1], scalar1=-1.0, scalar2=None, op0=ALU.mult),
             reads=["bt"], writes=["negb"])
        P.op("pool", lambda h: h.memset(Vt[:], 1.0), writes=[f"V_{kb}" for kb in range(NB)])
        for hh in range(2):
            P.op("pool", lambda h, hh=hh: h.memset(KT[hh][64:70, :], 1.0), writes=[f"KTaug{hh}_{kb}" for kb in range(NB)])
            for s in range(2):
                P.op("pool", lambda h, hh=hh, s=s: h.memset(Qaug[hh][s][64:70, :], 1.0), writes=[f"Qaug_aug{hh}_{s}"])
        def prep(jb):
            s2 = jb % 2
            blk = slice(jb * 512, (jb + 1) * 512)
            if ucache is not None and pp > 0:
                P.op("sp", lambda h, s2=s2, blk=blk: h.dma_start(out=uT[s2][:], in_=uc_v[:, :, blk]), reads=[f"ucache{jb}"], writes=[f"uT{s2}"], dma=True)
            else:
                for tt in range(4):
                    t = jb * 4 + tt
                    s = t % 2
                    c = 4 * tt
                    P.op("sp", lambda h, t=t, s=s: h.dma_start(out=x_sb[s][:], in_=x_d[t * 128:(t + 1) * 128, :]), writes=[f"x{s}"], dma=True)
                    P.op("dve", lambda h, s=s, c=c: h.scalar_tensor_tensor(out=sq[:], in0=x_sb[s][:], scalar=1.0, in1=x_sb[s][:],
                                                                           op0=ALU.mult, op1=ALU.mult, accum_out=st[:, c:c + 1]),
                         reads=[f"x{s}"], writes=["sq", f"ss{tt}"])
                    P.op("dve", lambda h, c=c: h.tensor_scalar(out=st[:, c + 1:c + 2], in0=st[:, c:c + 1], scalar1=1.0 / 1024, scalar2=EPS,
                                                               op0=ALU.mult, op1=ALU.add), reads=[f"ss{tt}"], writes=[f"rt{tt}"])
                    P.op("pool", lambda h, c=c: h.tensor_tensor(out=st[:, c + 2:c + 3], in0=st[:, c + 1:c + 2], in1=mhalf[:], op=ALU.pow),
                         reads=[f"rt{tt}", "mhalf"], writes=[f"rstd{tt}"])
                    P.op("dve", lambda h, s=s, c=c, tt=tt: h.tensor_scalar(out=usc[tt][:], in0=x_sb[s][:], scalar1=st[:, c + 2:c + 3], scalar2=None, op0=ALU.mult),
                         reads=[f"x{s}", f"rstd{tt}"], writes=[f"usc{tt}"])
                for tt in range(4):
                    tok = slice(tt * 128, (tt + 1) * 128)
                    for k in range(8):
                        ks = slice(k * 128, (k + 1) * 128)
                        P.op("pe", lambda h, ks=ks, tt=tt: h.transpose(out=ptr[:, ks], in_=usc[tt][:, ks], identity=ident[:]),
                             reads=[f"usc{tt}", "ident"], writes=["ptr"])
                    P.op("dve", lambda h, s2=s2, tok=tok: h.tensor_copy(out=uT[s2][:, :, tok], in_=ptr[:].rearrange("p (k t) -> p k t", k=8)),
                         reads=["ptr"], writes=[f"uT{s2}"])
                if ucache is not None:
                    P.op("sp", lambda h, s2=s2, blk=blk: h.dma_start(out=uc_v[:, :, blk], in_=uT[s2][:]), reads=[f"uT{s2}"], writes=[f"ucache{jb}"], dma=True)
            gi = 0
            for kind in ("q", "k", "g"):
                base = {"q": 0, "k": 128, "g": 384}[kind]
                pjb = pj[gi % 2]
                pjn = f"pj{gi % 2}"
                gi += 1
                for k in range(8):
                    P.op("pe", lambda h, k=k, base=base, pjb=pjb, s2=s2: h.matmul(
                        pjb[:, :], lhsT=wsc[:, k, base:base + 128], rhs=uT[s2][:, k, :], start=(k == 0), stop=(k == 7)),
                        reads=["wsc", f"uT{s2}"], writes=[pjn])
                if kind == "q":
                    P.op("dve", lambda h, pjb=pjb, s2=s2: h.tensor_scalar(out=Qaug[0][s2][0:64, :], in0=pjb[0:64, :], scalar1=0.125, scalar2=None, op0=ALU.mult),
                         reads=[pjn], writes=[f"Qaug_m0_{s2}"])
                    P.op("dve", lambda h, pjb=pjb: h.tensor_scalar(out=qst[64:128, :], in0=pjb[64:128, :], scalar1=0.125, scalar2=None, op0=ALU.mult),
                         reads=[pjn], writes=["qst"])
                    P.op("pool", lambda h, s2=s2: h.dma_start(out=Qaug[1][s2][0:64, :], in_=qst[64:128, :]), reads=["qst"], writes=[f"Qaug_m1_{s2}"], dma=True)
                elif kind == "k":
                    P.op("dve", lambda h, pjb=pjb, blk=blk: h.tensor_copy(out=KT[0][0:64, blk], in_=pjb[0:64, :]), reads=[pjn], writes=[f"KT0_{jb}"])
                    P.op("dve", lambda h, pjb=pjb: h.tensor_copy(out=kst[64:128, :], in_=pjb[64:128, :]), reads=[pjn], writes=["kst"])
                    P.op("pool", lambda h, blk=blk: h.dma_start(out=KT[1][0:64, blk], in_=kst[64:128, :]), reads=["kst"], writes=[f"KT1_{jb}"], dma=True)
                else:
                    P.op("act", lambda h, pjb=pjb: h.activation(out=eg[:], in_=pjb[:, :], func=AF.Exp, scale=-1.0), reads=[pjn], writes=["eg"])
                    P.op("act", lambda h: h.activation(out=eg2[:], in_=eg[:], func=AF.Ln, scale=1.0, bias=1.0), reads=["eg"], writes=["eg2"])
                    P.op("act", lambda h: h.activation(out=eg[:], in_=eg2[:], func=AF.Exp, scale=-1.0), reads=["eg2"], writes=["eg"])
                    P.op("dve", lambda h, pjb=pjb, s2=s2: h.tensor_tensor(out=sgf[s2][:], in0=pjb[:, :], in1=eg[:], op=ALU.mult),
                         reads=[pjn, "eg"], writes=[f"sg0_{s2}", f"sgf_{s2}"])
                    P.op("pool", lambda h, s2=s2: h.dma_start(out=sg[1][s2][:], in_=sgf[s2][64:128, :]), reads=[f"sgf_{s2}"], writes=[f"sg1_{s2}"], dma=True)
            pjb = pj[gi % 2]
            pjn = f"pj{gi % 2}"
            gi += 1
            for tt in range(4):
                tok = slice(tt * 128, (tt + 1) * 128)
                for k in range(8):
                    P.op("pe", lambda h, k=k, pjb=pjb, s2=s2, tok=tok: h.matmul(
                        pjb[:, tok], lhsT=uT[s2][:, k, tok], rhs=wsc[:, k, 256:384], start=(k == 0), stop=(k == 7)),
                        reads=["wsc", f"uT{s2}"], writes=[pjn])
            P.op("dve", lambda h, pjb=pjb, jb=jb: h.tensor_copy(out=Vt[:, jb * 4:(jb + 1) * 4, :, 0:64],
                                                                 in_=pjb[:].rearrange("p (t h d) -> p t h d", t=4, h=2)),
                 reads=[pjn], writes=[f"V_{jb}"])
            pjb = pj[gi % 2]
            pjn = f"pj{gi % 2}"
            gi += 1
            for k in range(8):
                P.op("pe", lambda h, k=k, pjb=pjb, s2=s2: h.matmul(
                    pjb[0:2, :], lhsT=wsc[:, k, 512:514], rhs=uT[s2][:, k, :], start=(k == 0), stop=(k == 7)),
                    reads=["wsc", f"uT{s2}"], writes=[pjn])
            P.op("act", lambda h, pjb=pjb: h.activation(out=e1[:], in_=pjb[0:2, :], func=AF.Exp, scale=-1.0, bias=bt[:, 1:2]),
                 reads=[pjn, "negb"], writes=["e1"])
            P.op("act", lambda h: h.activation(out=spl[:], in_=e1[:], func=AF.Ln, scale=1.0, bias=1.0), reads=["e1"], writes=["spl"])
            init = 0.0 if jb == 0 else cpos[1 - s2][:, 511:512]
            P.op("dve", lambda h, s2=s2, init=init: h.tensor_tensor_scan(out=cpos[s2][:], data0=ones2[:], data1=spl[:], initial=init,
                                                                         op0=ALU.mult, op1=ALU.add),
                 reads=["ones2", "spl", f"cpos{1 - s2}"], writes=[f"cpos{s2}"])
            P.op("dve", lambda h, s2=s2: h.tensor_copy(out=ST[:, 3, :], in_=cpos[s2][:]), reads=[f"cpos{s2}"], writes=["ST3"])
            P.op("dve", lambda h, s2=s2: h.tensor_tensor(out=r1[:], in0=cpos[s2][:], in1=ST[:, 3, :], op=ALU.subtract), reads=[f"cpos{s2}", "ST3"], writes=["r1"])
            P.op("dve", lambda h: h.tensor_copy(out=ST[:, 4, :], in_=r1[:]), reads=["r1"], writes=["ST4"])
            P.op("dve", lambda h: h.tensor_tensor(out=r2[:], in0=r1[:], in1=ST[:, 4, :], op=ALU.subtract), reads=["r1", "ST4"], writes=["r2"])
            P.op("dve", lambda h: h.tensor_copy(out=ST[:, 5, :], in_=r2[:]), reads=["r2"], writes=["ST5"])
            P.op("dve", lambda h: h.tensor_scalar(out=ST[:, 0:3, :], in0=ST[:, 3:6, :], scalar1=-1.0, scalar2=None, op0=ALU.mult),
                 reads=["ST3", "ST4", "ST5"], writes=["ST0"])
            ci = pp * NB + jb
            P.op("pool", lambda h, ci=ci: h.dma_start(out=cscr[ci], in_=ST[:]), reads=["ST0", "ST3", "ST4", "ST5"], writes=[f"cscr{ci}"], dma=True)
            for hh in range(2):
                P.op("pool", lambda h, ci=ci, hh=hh, s2=s2: h.dma_start(out=Qaug[hh][s2][64:67, :], in_=cscr[ci, hh, 0:3, :]),
                     reads=[f"cscr{ci}"], writes=[f"Qaug_aug{hh}_{s2}"], dma=True)
                P.op("pool", lambda h, ci=ci, hh=hh, blk=blk: h.dma_start(out=KT[hh][67:70, blk], in_=cscr[ci, hh, 3:6, :]),
                     reads=[f"cscr{ci}"], writes=[f"KTaug{hh}_{jb}"], dma=True)

        def attend(jb):
            s2 = jb % 2
            nkt = 4 * (jb + 1)
            items = [(hh, kt) for kt in range(nkt) for hh in range(2)]
            n = len(items)
            for i in range(n + 2):
                if i < n:
                    hh, kt = items[i]
                    r = kt - 4 * jb
                    c0 = max(r, 0) * 128
                    kb = kt // 4
                    sl = i % 3
                    P.op("pe", lambda h, hh=hh, kt=kt, c0=c0, sl=sl, s2=s2: h.matmul(
                        ps_s[sl][:, c0:512], lhsT=KT[hh][0:70, kt * 128:(kt + 1) * 128], rhs=Qaug[hh][s2][0:70, c0:512], start=True, stop=True),
                        reads=[f"KT{hh}_{kb}", f"KTaug{hh}_{kb}", f"Qaug_m{hh}_{s2}", f"Qaug_aug{hh}_{s2}"], writes=[f"ps_s{sl}"])
                    if r >= 0:
                        P.op("dve", lambda h, sl=sl, c0=c0: h.tensor_tensor(out=ps_s[sl][:, c0:c0 + 128], in0=ps_s[sl][:, c0:c0 + 128], in1=mneg[:], op=ALU.min),
                             reads=[f"ps_s{sl}", "mneg"], writes=[f"ps_s{sl}"])
                j = i - 2
                if j >= 0:
                    hh, kt = items[j]
                    r = kt - 4 * jb
                    c0 = max(r, 0) * 128
                    kb = kt // 4
                    sl = j % 3
                    P.op("act", lambda h, sl=sl, c0=c0: h.activation(out=Pt[sl][:, c0:512], in_=ps_s[sl][:, c0:512], func=AF.Exp),
                         reads=[f"ps_s{sl}"], writes=[f"Pt{sl}"])
                    P.op("pe", lambda h, hh=hh, kt=kt, c0=c0, sl=sl, nkt=nkt: h.matmul(
                        po[hh][0:65, c0:512], lhsT=Vt[:, kt, hh, 0:65], rhs=Pt[sl][:, c0:512], start=(kt == 0), stop=(kt == nkt - 1)),
                        reads=[f"V_{kb}", f"Pt{sl}"], writes=[f"po{hh}"])

        def finish(jb):
            s2 = jb % 2
            blk = slice(jb * 512, (jb + 1) * 512)
            for hh in range(2):
                P.op("act", lambda h, hh=hh: h.activation(out=rden[64:65, :], in_=po[hh][64:65, :], func=AF.Ln), reads=[f"po{hh}"], writes=["rden"])
                P.op("act", lambda h: h.activation(out=rden[64:65, :], in_=rden[64:65, :], func=AF.Exp, scale=-1.0), reads=["rden"], writes=["rden"])
                P.op("pe", lambda h: h.matmul(pj[0][0:64, :], lhsT=ones_f[64:65, 0:64], rhs=rden[64:65, :], start=True, stop=True),
                     reads=["ones_f", "rden"], writes=["pj0"])
                P.op("dve", lambda h, hh=hh, s2=s2: h.tensor_tensor(out=tnorm[:], in0=pj[0][0:64, :], in1=sg[hh][s2], op=ALU.mult),
                     reads=["pj0", f"sg{hh}_{s2}"], writes=["tnorm"])
                P.op("dve", lambda h, hh=hh, s2=s2: h.tensor_tensor(out=a_blk[hh][s2][:], in0=po[hh][0:64, :], in1=tnorm[:], op=ALU.mult),
                     reads=[f"po{hh}", "tnorm"], writes=[f"a_blk{hh}_{s2}"])
                row = (2 * pp + hh) * 64
                o = P.op("pool", lambda h, hh=hh, s2=s2, row=row, blk=blk: h.dma_start(out=aT_out[row:row + 64, blk], in_=a_blk[hh][s2][:]),
                         reads=[f"a_blk{hh}_{s2}"], dma=True)
                out_ops.append(o)

        P.replay(P.capture(lambda: prep(0)))
        for jb in range(NB):
            main = P.capture(lambda: attend(jb))
            side = P.capture(lambda: prep(jb + 1)) if jb + 1 < NB else []
            P.replay(main, side)
            finish(jb)
    return out_ops


def emit_rec(P, nc, es, S, n_pairs, uT_d, w_d, lb_d, og_d, yT_out, pfx):
    sb, ps = _mk(nc, es)
    NB = S // 512
    NH = 2 * n_pairs
    W = NH * 128
    ident = sb(pfx + "ident", [128, 128], BF16)
    tri = sb(pfx + "tri", [128, 512], F32)
    rmask = sb(pfx + "rmask", [128, 512], F32)
    mhalf = sb(pfx + "mhalf", [128, 1], F32)
    wst = sb(pfx + "wst", [128, 8, 256], F32)
    wbf = sb(pfx + "wbf", [128, 8, 1024], BF16)
    lbt = sb(pfx + "lbt", [128, 2, NH], F32)
    lbv = sb(pfx + "lbv", [128, 4, NH], F32)
    og_b = sb(pfx + "og_b", [128, 128], F32)
    uTb = [sb(pfx + f"uTb{i}", [128, 8, 512], BF16) for i in range(2)]
    sigT = sb(pfx + "sigT", [128, 512], F32)
    sgt1 = sb(pfx + "esgT", [128, 512], F32)
    fT = sb(pfx + "fT", [128, 512], F32)
    lfT = sb(pfx + "lfT", [128, 512], F32)
    bT = sb(pfx + "bT", [128, 512], F32)
    enbT = sb(pfx + "enbT", [128, 512], F32)
    edT = sb(pfx + "edT", [128, 512], F32)
    kT = sb(pfx + "kT", [128, 512], F32)
    KdT = sb(pfx + "KdT", [128, 512], BF16)
    ebT = [[sb(pfx + f"ebT{h}_{s}", [128, 512], F32) for s in range(2)] for h in range(2)]
    AT = [[sb(pfx + f"AT{h}_{s}", [128, 512], BF16) for s in range(2)] for h in range(2)]
    BT = [[sb(pfx + f"BT{h}_{s}", [128, 512], BF16) for s in range(2)] for h in range(2)]
    Kd = [[sb(pfx + f"Kd{h}_{s}", [128, 4, 128], BF16) for s in range(2)] for h in range(2)]
    inp_sb = [sb(pfx + f"inp_sb{s}", [128, 4, 256], BF16) for s in range(2)]
    sgt = [sb(pfx + f"sgt{s}", [128, 4, 256], BF16) for s in range(2)]
    eg = sb(pfx + "eg", [128, 256], F32)
    eg2 = sb(pfx + "eg2", [128, 256], F32)
    scT = sb(pfx + "scT", [128, 512], BF16)
    stf = [sb(pfx + f"stf{h}", [128, 128], F32) for h in range(2)]
    stb = [[sb(pfx + f"stb{h}_{s}", [128, 128], BF16) for s in range(2)] for h in range(2)]
    osb = sb(pfx + "osb", [128, 128], F32)
    sq = sb(pfx + "sq", [128, 128], BF16)
    st = sb(pfx + "st", [128, 4], F32)
    t1 = sb(pfx + "t1", [128, 128], F32)
    ybf = sb(pfx + "ybf", [128, 128], BF16)
    yT_blk = [[sb(pfx + f"yT_blk{h}_{s}", [128, 512], BF16) for s in range(2)] for h in range(2)]
    pjf = ps(pfx + "pjf", [128, 512], F32)
    pjt = ps(pfx + "pjt", [128, 512], F32)
    psc = ps(pfx + "psc", [128, 512], F32)
    po = [ps(pfx + f"po{i}", [128, 512], F32) for i in range(2)]
    pst = ps(pfx + "pst", [128, 512], F32)
    ptr = ps(pfx + "ptr", [128, 1024], BF16)
    pty = ps(pfx + "pty", [128, 1024], BF16)

    make_ident(P, ident)
    P.op("pool", lambda h: h.memset(tri[:], 1.0), writes=["tri"])
    for half in range(2):
        P.op("pool", lambda h, half=half: h.affine_select(
            out=tri[half * 64:(half + 1) * 64, :], in_=tri[half * 64:(half + 1) * 64, :], pattern=[[0, 8], [1, 64]],
            compare_op=ALU.is_ge, fill=0.0, base=0, channel_multiplier=-1), reads=["tri"], writes=["tri"])
    P.op("pool", lambda h: h.memset(rmask[:], 1.0), writes=["rmask"])
    P.op("pool", lambda h: h.memset(rmask[:].rearrange("p (c t) -> p c t", t=64)[:, :, 0:1], 0.0), reads=["rmask"], writes=["rmask"])
    P.op("pool", lambda h: h.memset(mhalf[:], -0.5), writes=["mhalf"])
    P.op("sp", lambda h: h.dma_start(out=og_b[:], in_=og_d.partition_broadcast(128)), writes=["og_b"], dma=True)
    P.op("sp", lambda h: h.dma_start(out=lbt[:], in_=lb_d), writes=["lbt"], dma=True)
    P.op("dve", lambda h: h.tensor_tensor(out=lbv[:, 0, :], in0=lbt[:, 0, :], in1=lbt[:, 1, :], op=ALU.subtract), reads=["lbt"], writes=["lbdiff"])
    P.op("act", lambda h: h.activation(out=lbv[:, 0, :], in_=lbv[:, 0, :], func=AF.Exp), reads=["lbdiff"], writes=["lbdiff"])
    P.op("act", lambda h: h.activation(out=lbv[:, 0, :], in_=lbv[:, 0, :], func=AF.Ln, scale=1.0, bias=1.0), reads=["lbdiff"], writes=["lbdiff"])
    P.op("act", lambda h: h.activation(out=lbv[:, 1, :], in_=lbv[:, 0, :], func=AF.Exp, scale=-1.0), reads=["lbdiff"], writes=["lb"])
    P.op("dve", lambda h: h.tensor_scalar(out=lbv[:, 2, :], in0=lbv[:, 1, :], scalar1=-1.0, scalar2=1.0, op0=ALU.mult, op1=ALU.add),
         reads=["lb"], writes=["oml"])
    P.op("dve", lambda h: h.tensor_scalar(out=lbv[:, 3, :], in0=lbv[:, 1, :], scalar1=-1.0, scalar2=None, op0=ALU.add), reads=["lb"], writes=["noml"])
    w_v = w_d.rearrange("(k p) n -> p k n", p=128)
    uT_v = uT_d.rearrange("(k p) t -> p k t", p=128)
    out_ops = []
    for rp in range(n_pairs):
        for g in range(4):
            P.op("sp", lambda h, g=g, rp=rp: h.dma_start(out=wst[:], in_=w_v[:, :, g * W + rp * 256:g * W + (rp + 1) * 256]), writes=["wst"], dma=True)
            P.op("dve", lambda h, g=g: h.tensor_copy(out=wbf[:, :, g * 256:(g + 1) * 256], in_=wst[:]), reads=["wst"], writes=["wbf"])
        for hh in range(2):
            P.op("pool", lambda h, hh=hh: h.memset(stf[hh][:], 0.0), writes=[f"stf{hh}"])
            P.op("pool", lambda h, hh=hh: h.memset(stb[hh][0][:], 0.0), writes=[f"stb{hh}_0"])

        def prep(jb):
            s2 = jb % 2
            blk = slice(jb * 512, (jb + 1) * 512)
            P.op("sp", lambda h: h.dma_start(out=uTb[s2][:], in_=uT_v[:, :, blk]), writes=[f"uTb{s2}"], dma=True)
            for tt in range(4):
                tok = slice(tt * 128, (tt + 1) * 128)
                for k in range(8):
                    P.op("pe", lambda h, k=k, tok=tok: h.matmul(pjt[:], lhsT=uTb[s2][:, k, tok], rhs=wbf[:, k, 512:1024], start=(k == 0), stop=(k == 7)),
                         reads=["wbf", f"uTb{s2}"], writes=["pjt"])
                P.op("dve", lambda h, tt=tt: h.tensor_copy(out=inp_sb[s2][:, tt, :], in_=pjt[:, 0:256]), writes=["pjt", f"inp{s2}"])
                P.op("act", lambda h: h.activation(out=eg[:], in_=pjt[:, 256:512], func=AF.Exp, scale=-1.0), writes=["pjt", "eg"])
                P.op("act", lambda h: h.activation(out=eg2[:], in_=eg[:], func=AF.Ln, scale=1.0, bias=1.0), reads=["eg"], writes=["eg2"])
                P.op("act", lambda h: h.activation(out=eg[:], in_=eg2[:], func=AF.Exp, scale=-1.0), reads=["eg2"], writes=["eg"])
                P.op("dve", lambda h, tt=tt: h.tensor_tensor(out=sgt[s2][:, tt, :], in0=pjt[:, 256:512], in1=eg[:], op=ALU.mult),
                     reads=["eg"], writes=["pjt", f"sgt{s2}"])
            for hh in range(2):
                hd = 2 * rp + hh
                lb_c, oml_c, noml_c = lbv[:, 1, hd:hd + 1], lbv[:, 2, hd:hd + 1], lbv[:, 3, hd:hd + 1]
                fcol = 256 + hh * 128
                qcol = hh * 128
                for k in range(8):
                    P.op("pe", lambda h, k=k, fcol=fcol: h.matmul(pjf[:], lhsT=wbf[:, k, fcol:fcol + 128], rhs=uTb[s2][:, k, :], start=(k == 0), stop=(k == 7)),
                         reads=["wbf", f"uTb{s2}"], writes=["pjf"])
                P.op("act", lambda h: h.activation(out=sgt1[:], in_=pjf[:], func=AF.Exp, scale=-1.0), writes=["pjf", "sgt1"])
                for k in range(8):
                    P.op("pe", lambda h, k=k, qcol=qcol: h.matmul(pjf[:], lhsT=wbf[:, k, qcol:qcol + 128], rhs=uTb[s2][:, k, :], start=(k == 0), stop=(k == 7)),
                         reads=["wbf", f"uTb{s2}"], writes=["pjf"])
                P.op("act", lambda h: h.activation(out=sigT[:], in_=sgt1[:], func=AF.Ln, scale=1.0, bias=1.0), reads=["sgt1"], writes=["sigT"])
                P.op("act", lambda h: h.activation(out=sigT[:], in_=sigT[:], func=AF.Exp, scale=-1.0), reads=["sigT"], writes=["sigT"])
                P.op("dve", lambda h, oml_c=oml_c, lb_c=lb_c: h.tensor_scalar(out=fT[:], in0=sigT[:], scalar1=oml_c, scalar2=lb_c, op0=ALU.mult, op1=ALU.add),
                     reads=["sigT", "oml", "lb"], writes=["fT"])
                P.op("act", lambda h: h.activation(out=lfT[:], in_=fT[:], func=AF.Ln), reads=["fT"], writes=["lfT"])
                P.op("dve", lambda h: h.tensor_tensor_scan(out=bT[:], data0=rmask[:], data1=lfT[:], initial=0.0, op0=ALU.mult, op1=ALU.add),
                     reads=["rmask", "lfT"], writes=["bT"])
                P.op("act", lambda h, hh=hh: h.activation(out=ebT[hh][s2][:], in_=bT[:], func=AF.Exp), reads=["bT"], writes=[f"ebT{hh}_{s2}"])
                P.op("act", lambda h: h.activation(out=enbT[:], in_=bT[:], func=AF.Exp, scale=-1.0), reads=["bT"], writes=["enbT"])
                P.op("dve", lambda h, hh=hh: h.tensor_tensor(out=AT[hh][s2][:], in0=pjf[:], in1=ebT[hh][s2][:], op=ALU.mult),
                     reads=[f"ebT{hh}_{s2}"], writes=["pjf", f"AT{hh}_{s2}"])
                P.op("dve", lambda h, oml_c=oml_c, noml_c=noml_c: h.tensor_scalar(out=kT[:], in0=sigT[:], scalar1=noml_c, scalar2=oml_c, op0=ALU.mult, op1=ALU.add),
                     reads=["sigT", "oml", "noml"], writes=["kT"])
                P.op("dve", lambda h, hh=hh: h.tensor_tensor(out=BT[hh][s2][:], in0=kT[:], in1=enbT[:], op=ALU.mult),
                     reads=["kT", "enbT"], writes=[f"BT{hh}_{s2}"])
                for c in range(8):
                    cs = slice(c * 64, (c + 1) * 64)
                    P.op("act", lambda h, cs=cs, c=c: h.activation(out=edT[:, cs], in_=bT[:, cs], func=AF.Exp, scale=-1.0,
                                                                   bias=bT[:, c * 64 + 63:c * 64 + 64]), reads=["bT"], writes=["edT"])
                P.op("dve", lambda h: h.tensor_tensor(out=KdT[:], in0=kT[:], in1=edT[:], op=ALU.mult), reads=["kT", "edT"], writes=["KdT"])
                for tt in range(4):
                    tok = slice(tt * 128, (tt + 1) * 128)
                    P.op("pe", lambda h, tok=tok: h.transpose(out=ptr[:, tok], in_=KdT[:, tok], identity=ident[:]), reads=["KdT", "ident"], writes=["ptr"])
                P.op("dve", lambda h, hh=hh: h.tensor_copy(out=Kd[hh][s2][:], in_=ptr[:, 0:512].rearrange("p (t d) -> p t d", t=4)),
                     writes=["ptr", f"Kd{hh}_{s2}"])

        def recur(jb):
            s2 = jb % 2
            blk = slice(jb * 512, (jb + 1) * 512)
            for c_ in range(8):
                for hh in range(2):
                    pr_ = slice(64 * (c_ % 2), 64 * (c_ % 2) + 64)
                    sl_ = slice(((c_ // 2) * 2 + hh) * 64, ((c_ // 2) * 2 + hh + 1) * 64)
                    cs_ = slice(c_ * 64, (c_ + 1) * 64)
                    P.op("pe", lambda h, hh=hh, pr_=pr_, sl_=sl_, cs_=cs_: h.matmul(psc[pr_, sl_], lhsT=BT[hh][s2][:, cs_], rhs=AT[hh][s2][:, cs_],
                                                                                start=True, stop=True),
                         reads=[f"BT{hh}_{s2}", f"AT{hh}_{s2}"], writes=["psc"])
            P.op("dve", lambda h: h.tensor_tensor(out=scT[:], in0=psc[:], in1=tri[:], op=ALU.mult), reads=["tri"], writes=["psc", "scT"])
            for tt_ in range(4):
                main_ = P.capture(lambda: (chunk(jb, s2, 2 * tt_), chunk(jb, s2, 2 * tt_ + 1)))
                side_ = P.capture(lambda: norm(jb, s2, tt_ - 1)) if tt_ >= 1 else []
                P.replay(main_, side_)
            norm(jb, s2, 3)
            for hh in range(2):
                row = (2 * rp + hh) * 128
                o = P.op("pool", lambda h, hh=hh, row=row: h.dma_start(out=yT_out[row:row + 128, blk], in_=yT_blk[hh][s2][:]),
                         reads=[f"yT{hh}_{s2}"], dma=True)
                out_ops.append(o)

        def chunk(jb, s2, c):
            if True:
                tt, half = c // 2, c % 2
                p0 = 64 * half
                pr = slice(p0, p0 + 64)
                cs = slice(c * 64, (c + 1) * 64)
                par = tt % 2
                cgl = jb * 8 + c
                cur, nxt = cgl % 2, (cgl + 1) % 2
                for hh in range(2):
                    hc = slice(hh * 128, (hh + 1) * 128)
                    sc = slice(hh * 64, (hh + 1) * 64)
                    sl = slice((tt * 2 + hh) * 64, (tt * 2 + hh + 1) * 64)
                    P.op("pe", lambda h, hh=hh, hc=hc, sl=sl: h.matmul(po[par][pr, hc], lhsT=scT[pr, sl], rhs=inp_sb[s2][pr, tt, hc], start=True, stop=False),
                         reads=["scT", f"inp{s2}"], writes=[f"po{par}"])
                    P.op("pe", lambda h, hh=hh, hc=hc: h.matmul(po[par][pr, hc], lhsT=AT[hh][s2][:, cs], rhs=stb[hh][cur][:], start=False, stop=True),
                         reads=[f"AT{hh}_{s2}", f"stb{hh}_{cur}"], writes=[f"po{par}"])
                    P.op("pe", lambda h, hh=hh, hc=hc: h.matmul(pst[:, hc], lhsT=Kd[hh][s2][pr, tt, :], rhs=inp_sb[s2][pr, tt, hc], start=True, stop=True),
                         reads=[f"Kd{hh}_{s2}", f"inp{s2}"], writes=["pst"])
                    P.op("dve", lambda h, hh=hh, hc=hc: h.scalar_tensor_tensor(
                        out=stf[hh][:], in0=stf[hh][:], scalar=ebT[hh][s2][:, c * 64 + 63:c * 64 + 64], in1=pst[:, hc], op0=ALU.mult, op1=ALU.add),
                        reads=[f"ebT{hh}_{s2}"], writes=["pst", f"stf{hh}"])
                    P.op("act", lambda h, hh=hh: h.copy(out=stb[hh][nxt][:], in_=stf[hh][:]), reads=[f"stf{hh}"], writes=[f"stb{hh}_{nxt}"])

        def norm(jb, s2, tt):
            if True:
                if True:
                    par = tt % 2
                    tok = slice(tt * 128, (tt + 1) * 128)
                    for hh in range(2):
                        hc = slice(hh * 128, (hh + 1) * 128)
                        yc = slice(hh * 128, (hh + 1) * 128)
                        P.op("act", lambda h, hc=hc: h.copy(out=osb[:], in_=po[par][:, hc]), writes=[f"po{par}", "osb"])
                        P.op("dve", lambda h: h.scalar_tensor_tensor(out=sq[:], in0=osb[:], scalar=1.0, in1=osb[:], op0=ALU.mult, op1=ALU.mult,
                                                                     accum_out=st[:, 0:1]), reads=["osb"], writes=["sq", "ss"])
                        P.op("dve", lambda h: h.tensor_scalar(out=st[:, 1:2], in0=st[:, 0:1], scalar1=1.0 / 128, scalar2=EPS, op0=ALU.mult, op1=ALU.add),
                             reads=["ss"], writes=["rt"])
                        P.op("pool", lambda h: h.tensor_tensor(out=st[:, 2:3], in0=st[:, 1:2], in1=mhalf[:], op=ALU.pow), reads=["rt", "mhalf"], writes=["rstd"])
                        P.op("dve", lambda h: h.scalar_tensor_tensor(out=t1[:], in0=osb[:], scalar=st[:, 2:3], in1=og_b[:], op0=ALU.mult, op1=ALU.mult),
                             reads=["osb", "rstd", "og_b"], writes=["t1"])
                        P.op("dve", lambda h, hc=hc: h.tensor_tensor(out=ybf[:], in0=t1[:], in1=sgt[s2][:, tt, hc], op=ALU.mult),
                             reads=["t1", f"sgt{s2}"], writes=["ybf"])
                        P.op("pe", lambda h, yc=yc: h.transpose(out=pty[:, yc], in_=ybf[:], identity=ident[:]), reads=["ybf", "ident"], writes=["pty"])
                        P.op("act", lambda h, hh=hh, yc=yc: h.copy(out=yT_blk[hh][s2][:, tok], in_=pty[:, yc]), writes=["pty", f"yT{hh}_{s2}"])

        P.replay(P.capture(lambda: prep(0)))
        for jb in range(NB):
            main = P.capture(lambda: recur(jb))
            side = P.capture(lambda: prep(jb + 1)) if jb + 1 < NB else []
            P.replay(main, side)
    return out_ops


def _one_phase(fn):
    nc = bass.Bass("TRN2", target_bir_lowering=False)
    with ExitStack() as es:
        P = Prog(nc, es)
        oo = fn(P, nc, es)
        P.emit(final_wait_ops=oo)
    return nc


def build_att(S):
    def fn(P, nc, es):
        NB = S // 512
        D = lambda n, sh, dt, k="ExternalInput": nc.dram_tensor(n, sh, dt, kind=k).ap()
        x_d, w_d, gp_d, bf_d = D("xin", [S, 1024], F32), D("watt", [1024, 1028], F32), D("gpre", [128, 8], F32), D("bf", [2, 2, 1], F32)
        aT_out = D("aTout", [256, S], BF16, "ExternalOutput")
        cscr = nc.dram_tensor("cscr", [2 * NB, 2, 6, 512], BF16).ap()
        return emit_att(P, nc, es, S, 2, x_d, w_d, gp_d, bf_d, aT_out, cscr, "a_")
    return _one_phase(fn)


def build_rec(S):
    def fn(P, nc, es):
        D = lambda n, sh, dt, k="ExternalInput": nc.dram_tensor(n, sh, dt, kind=k).ap()
        uT_d, w_d, lb_d, og_d = D("uT", [1024, S], BF16), D("wrec", [1024, 1024], F32), D("lbr", [128, 2, 2], F32), D("og", [1, 128], F32)
        yT_out = D("yTout", [256, S], BF16, "ExternalOutput")
        return emit_rec(P, nc, es, S, 1, uT_d, w_d, lb_d, og_d, yT_out, "c_")
    return _one_phase(fn)


def build_post(T, emit_u):
    def fn(P, nc, es):
        D = lambda n, sh, dt, k="ExternalInput": nc.dram_tensor(n, sh, dt, kind=k).ap()
        aT_d, x_d, pT_d = D("aT", [1024, T], BF16), D("xin", [T, 1024], F32), D("pT", [256, T], F32)
        wo_d, wg_d, wp_d = D("wo", [1024, 1024], F32), D("wg", [1024, 1024], F32), D("wp", [256, 1024], F32)
        gpost_d = D("gpost", [1, 1024], F32)
        gpre_d = D("gpre", [1, 1024], F32) if emit_u else None
        h_out = D("hout", [T, 1024], F32, "ExternalOutput")
        uT_out = D("uTout", [1024, T], BF16, "ExternalOutput") if emit_u else None
        return emit_post(P, nc, es, T, emit_u, aT_d, x_d, pT_d, wo_d, wg_d, wp_d, gpost_d, gpre_d, h_out, uT_out, "b_")
    return _one_phase(fn)


def kernel(x, p, norm_pre, norm_post, att_w_in, att_b_f, att_w_out, rec_w_in, rec_lb,
           rec_out_norm, rec_w_out, ple_w_proj, ple_w_gate):
    f32 = np.float32
    x = np.asarray(x, f32); p = np.asarray(p, f32)
    norm_pre = np.asarray(norm_pre, f32); norm_post = np.asarray(norm_post, f32)
    att_w_in = np.asarray(att_w_in, f32); att_b_f = np.asarray(att_b_f, f32); att_w_out = np.asarray(att_w_out, f32)
    rec_w_in = np.asarray(rec_w_in, f32); rec_lb = np.asarray(rec_lb, f32); rec_out_norm = np.asarray(rec_out_norm, f32)
    rec_w_out = np.asarray(rec_w_out, f32); ple_w_proj = np.asarray(ple_w_proj, f32); ple_w_gate = np.asarray(ple_w_gate, f32)
    B, S, D = x.shape
    NCORE = 8
    G = NCORE // B
    TS = S // G
    cores = list(range(NCORE))
    ca = np.ascontiguousarray

    maps = []
    for c in cores:
        b, g = divmod(c, G)
        w = att_w_in[0]
        watt = np.concatenate([w[:, g * 256:(g + 1) * 256], w[:, 1024 + g * 256:1024 + (g + 1) * 256],
                               w[:, 2048 + g * 256:2048 + (g + 1) * 256], w[:, 3072 + g * 256:3072 + (g + 1) * 256],
                               w[:, 4096 + g * 4:4096 + (g + 1) * 4]], axis=1)
        maps.append({"xin": ca(x[b]), "watt": ca(watt), "gpre": ca(norm_pre[0].reshape(8, 128).T),
                     "bf": ca(att_b_f[0][g * 4:(g + 1) * 4].reshape(2, 2, 1))})
    r1 = run_bass_kernel_spmd(build_att(S), maps, core_ids=cores).results
    aT = [np.concatenate([r1[b * G + g]["aTout"] for g in range(G)], axis=0) for b in range(B)]

    maps = []
    for c in cores:
        b, g = divmod(c, G)
        ts = slice(g * TS, (g + 1) * TS)
        maps.append({"aT": ca(aT[b][:, ts]), "xin": ca(x[b, ts]), "pT": ca(p[0, b, ts].T), "wo": ca(att_w_out[0]),
                     "wg": ca(ple_w_gate[0]), "wp": ca(ple_w_proj[0]), "gpost": ca(norm_post[0][None]),
                     "gpre": ca(norm_pre[1][None])})
    r2 = run_bass_kernel_spmd(build_post(TS, True), maps, core_ids=cores).results
    uT = [np.concatenate([r2[b * G + g]["uTout"] for g in range(G)], axis=1) for b in range(B)]

    maps = []
    for c in cores:
        b, g = divmod(c, G)
        w = rec_w_in[0]
        wrec = np.concatenate([w[:, k * 1024 + g * 256:k * 1024 + (g + 1) * 256] for k in range(4)], axis=1)
        lbr = rec_lb[:, g * 256:(g + 1) * 256].reshape(2, 2, 128).transpose(2, 0, 1)
        maps.append({"uT": ca(uT[b]), "wrec": ca(wrec), "lbr": ca(lbr), "og": ca(rec_out_norm[0][None])})
    r3 = run_bass_kernel_spmd(build_rec(S), maps, core_ids=cores).results
    yT = [np.concatenate([r3[b * G + g]["yTout"] for g in range(G)], axis=0) for b in range(B)]

    maps = []
    for c in cores:
        b, g = divmod(c, G)
        ts = slice(g * TS, (g + 1) * TS)
        maps.append({"aT": ca(yT[b][:, ts]), "xin": ca(r2[c]["hout"]), "pT": ca(p[1, b, ts].T), "wo": ca(rec_w_out[0]),
                     "wg": ca(ple_w_gate[1]), "wp": ca(ple_w_proj[1]), "gpost": ca(norm_post[1][None])})
    r4 = run_bass_kernel_spmd(build_post(TS, False), maps, core_ids=cores).results
    out = np.empty((B, S, D), f32)
    for c in cores:
        b, g = divmod(c, G)
        out[b, g * TS:(g + 1) * TS] = r4[c]["hout"]
    return out
```
